# Optimizing a Trainium2 kernel written in Bass

```python
import math
import jax
import jax.numpy as jnp
from jax import lax
import numpy as np

D_MODEL = 2048
BATCH = 1
SEQ = 8192
DEPTH = 4

GRID_W = 64
CTX_LEN = 256
N_MIXERS = 4
NORM_EPS = 1e-6
ADA_CHUNKS = 6
ROPE_THETA = 10000.0
ROPE_DIM = 64
ATTN_BLOCK = 128

SSD_D_INNER = 2 * D_MODEL
SSD_HEAD_DIM = 64
SSD_N_HEADS = SSD_D_INNER // SSD_HEAD_DIM
SSD_GROUPS = 8
SSD_HEADS_PER_GROUP = SSD_N_HEADS // SSD_GROUPS
SSD_STATE = 128
SSD_CONV = 4
SSD_CHUNK = 128
SSD_CONV_DIM = SSD_D_INNER + 2 * SSD_GROUPS * SSD_STATE
SSD_IN_DIM = SSD_D_INNER + SSD_CONV_DIM + 2 * SSD_N_HEADS

LRU_WIDTH = D_MODEL
LRU_BLOCKS = 8
LRU_BLOCK = LRU_WIDTH // LRU_BLOCKS
LRU_CONV = 4
LRU_C = 8.0

MLA_HEADS = 16
MLA_Q_RANK = 768
MLA_KV_RANK = 512
MLA_NOPE = 128
MLA_ROPE = ROPE_DIM
MLA_V = 128
MLA_QK = MLA_NOPE + MLA_ROPE

SWA_Q_HEADS = 32
SWA_KV_HEADS = 8
SWA_GROUP = SWA_Q_HEADS // SWA_KV_HEADS
SWA_HEAD_DIM = ROPE_DIM
SWA_WINDOW = 128

FFN_HIDDEN = 5632
FFN_CONV = 3

kernel_name = "hybrid_interleaved_latent_diffusion_backbone"


def rmsnorm(x, g):
    xf = x.astype(jnp.float32)
    y = xf * lax.rsqrt(jnp.mean(xf * xf, axis=-1, keepdims=True) + NORM_EPS)
    return (y * g.astype(jnp.float32)).astype(x.dtype)


def modulate(h, shift, scale):
    return h * (1.0 + scale) + shift


def dwconv(x, w, b):
    k, ch = w.shape
    left = (k - 1) // 2
    y = lax.conv_general_dilated(x, w[:, None, :].astype(x.dtype), window_strides=(1,),
                                 padding=[(left, k - 1 - left)],
                                 dimension_numbers=('NWC', 'WIO', 'NWC'),
                                 feature_group_count=ch)
    return y + b


def axial_rope(n_tokens):
    rows = n_tokens // GRID_W
    row = jnp.repeat(jnp.arange(rows, dtype=jnp.float32), GRID_W)
    col = jnp.tile(jnp.arange(GRID_W, dtype=jnp.float32), rows)
    n_freq = ROPE_DIM // 4
    inv = ROPE_THETA ** (-jnp.arange(n_freq, dtype=jnp.float32) / n_freq)
    ang = jnp.concatenate([row[:, None] * inv, col[:, None] * inv], axis=-1)
    return jnp.cos(ang), jnp.sin(ang)


def apply_rope(x, cos, sin):
    half = x.shape[-1] // 2
    shape = (cos.shape[0],) + (1,) * (x.ndim - 3) + (half,)
    cos, sin = cos.reshape(shape), sin.reshape(shape)
    xf = x.astype(jnp.float32)
    x1, x2 = xf[..., :half], xf[..., half:]
    return jnp.concatenate([x1 * cos - x2 * sin, x2 * cos + x1 * sin], axis=-1).astype(x.dtype)


def softmax_attend(segments, scale, sink=None):
    logits = []
    for q, k, v, mask in segments:
        s = jnp.einsum('bqhgd,bkhd->bhgqk', q, k).astype(jnp.float32) * scale
        if mask is not None:
            s = jnp.where(mask, s, -jnp.inf)
        logits.append(s)
    s = jnp.concatenate(logits, axis=-1)
    m = jnp.max(s, axis=-1, keepdims=True)
    if sink is not None:
        sk = sink.astype(jnp.float32).reshape(1, s.shape[1], s.shape[2], 1, 1)
        m = jnp.maximum(m, sk)
        p = jnp.exp(s - m)
        denom = jnp.sum(p, axis=-1, keepdims=True) + jnp.exp(sk - m)
    else:
        p = jnp.exp(s - m)
        denom = jnp.sum(p, axis=-1, keepdims=True)
    v_all = jnp.concatenate([seg[2] for seg in segments], axis=1)
    w = (p / denom).astype(v_all.dtype)
    return jnp.einsum('bhgqk,bkhd->bqhgd', w, v_all)


def to_blocks(a):
    bsz, t = a.shape[:2]
    return jnp.moveaxis(a.reshape((bsz, t // ATTN_BLOCK, ATTN_BLOCK) + a.shape[2:]), 1, 0)


def from_blocks(o):
    nb, bsz, blk = o.shape[:3]
    return jnp.moveaxis(o, 0, 1).reshape(bsz, nb * blk, -1)


def linear_scan(a, b, h0):
    b = b.at[:, 0].add(a[:, 0] * h0)

    def combine(l, r):
        return l[0] * r[0], r[0] * l[1] + r[1]

    return lax.associative_scan(combine, (a, b), axis=1)[1]


def segsum(a):
    t = a.shape[-1]
    cs = jnp.cumsum(a, axis=-1)
    diff = cs[..., :, None] - cs[..., None, :]
    return jnp.where(jnp.tril(jnp.ones((t, t), dtype=bool)), diff, -jnp.inf)


def ssd_chunked(xs, adt, bm, cm, h0):
    bsz, t, g, e, p = xs.shape
    n = bm.shape[-1]
    nc = t // SSD_CHUNK
    xs = xs.reshape(bsz, nc, SSD_CHUNK, g, e, p)
    bm = bm.reshape(bsz, nc, SSD_CHUNK, g, n)
    cm = cm.reshape(bsz, nc, SSD_CHUNK, g, n)
    a = adt.astype(jnp.float32).reshape(bsz, nc, SSD_CHUNK, g, e).transpose(0, 3, 4, 1, 2)
    a_cs = jnp.cumsum(a, axis=-1)
    decay = jnp.exp(segsum(a))
    cb = jnp.einsum('bclgn,bcsgn->bcgls', cm, bm)
    y_diag = jnp.einsum('bcgls,bgecls,bcsgep->bclgep', cb, decay, xs)
    decay_states = jnp.exp(a_cs[..., -1:] - a_cs)
    states = jnp.einsum('bclgn,bgecl,bclgep->bcgepn', bm, decay_states, xs)
    states = jnp.concatenate([h0[:, None].astype(states.dtype), states], axis=1)
    chunk_decay = jnp.exp(segsum(jnp.pad(a_cs[..., -1], ((0, 0), (0, 0), (0, 0), (1, 0)))))
    new_states = jnp.einsum('bgezc,bcgepn->bzgepn', chunk_decay, states)
    states, final = new_states[:, :-1], new_states[:, -1]
    y_off = jnp.einsum('bclgn,bcgepn,bgecl->bclgep', cm, states, jnp.exp(a_cs))
    return (y_diag + y_off).reshape(bsz, t, g, e, p), final


def ssd_mixer(hc, hl, need_ctx, w_in, conv_w, conv_b, dt_bias, a_log, d_skip, norm_w, w_out):
    def project(h):
        bsz, t, _ = h.shape
        rest = h @ w_in[:, SSD_D_INNER:]
        xbc = jax.nn.silu(dwconv(rest[..., :SSD_CONV_DIM], conv_w, conv_b))
        gn = SSD_GROUPS * SSD_STATE
        xs = xbc[..., :SSD_D_INNER].reshape(bsz, t, SSD_GROUPS, SSD_HEADS_PER_GROUP, SSD_HEAD_DIM)
        bm = xbc[..., SSD_D_INNER:SSD_D_INNER + gn].reshape(bsz, t, SSD_GROUPS, SSD_STATE)
        cm = xbc[..., SSD_D_INNER + gn:].reshape(bsz, t, SSD_GROUPS, SSD_STATE)
        dt = rest[..., SSD_CONV_DIM:].reshape(bsz, t, 2, SSD_N_HEADS)
        return xs, bm, cm, dt

    def run(xs, bm, cm, dt, a, h0, reverse):
        if reverse:
            xs, bm, cm, dt = (jnp.flip(v, axis=1) for v in (xs, bm, cm, dt))
        y, h_t = ssd_chunked(xs * dt[..., None], dt * a, bm, cm, h0)
        if reverse:
            y = jnp.flip(y, axis=1)
        return y, h_t

    def gated_out(y, h):
        bsz, t, _ = h.shape
        z = h @ w_in[:, :SSD_D_INNER]
        g = (y.reshape(bsz, t, SSD_D_INNER) * jax.nn.silu(z)).astype(jnp.float32)
        g = g.reshape(bsz, t, SSD_GROUPS, -1)
        g = g * lax.rsqrt(jnp.mean(g * g, axis=-1, keepdims=True) + NORM_EPS)
        return (g.reshape(bsz, t, SSD_D_INNER) * norm_w).astype(h.dtype) @ w_out

    bsz, n_ctx = hc.shape[:2]
    n_lat = hl.shape[1]
    xs_c, b_c, c_c, dtr_c = project(hc)
    xs_l, b_l, c_l, dtr_l = project(hl)
    ys_c, ys_l = [], []
    for d in range(2):
        a = -jnp.exp(a_log[d].astype(jnp.float32)).reshape(SSD_GROUPS, SSD_HEADS_PER_GROUP)
        dt_c = jax.nn.softplus(dtr_c[:, :, d].astype(jnp.float32) + dt_bias[d]).reshape(
            bsz, n_ctx, SSD_GROUPS, SSD_HEADS_PER_GROUP)
        dt_l = jax.nn.softplus(dtr_l[:, :, d].astype(jnp.float32) + dt_bias[d]).reshape(
            bsz, n_lat, SSD_GROUPS, SSD_HEADS_PER_GROUP)
        h0 = jnp.zeros((bsz, SSD_GROUPS, SSD_HEADS_PER_GROUP, SSD_HEAD_DIM, SSD_STATE), jnp.float32)
        y_c, h_c = run(xs_c, b_c, c_c, dt_c, a, h0, d == 1)
        y_l, _ = run(xs_l, b_l, c_l, dt_l, a, h_c, d == 1)
        skip = d_skip[d].reshape(SSD_GROUPS, SSD_HEADS_PER_GROUP, 1)
        ys_c.append(y_c + skip * xs_c)
        ys_l.append(y_l + skip * xs_l)
    o_l = gated_out(ys_l[0] + ys_l[1], hl)
    o_c = gated_out(ys_c[0] + ys_c[1], hc) if need_ctx else None
    return o_c, o_l


def rglru_mixer(hc, hl, need_ctx, w_in, conv_w, conv_b, gate_w, gate_b, lam, w_out):
    def x_branch(h):
        return dwconv(h @ w_in[:, LRU_WIDTH:], conv_w, conv_b)

    def gate_branch(h):
        return jax.nn.gelu(h @ w_in[:, :LRU_WIDTH], approximate=True)

    def gates(u, d):
        bsz, t, _ = u.shape
        ub = u.reshape(bsz, t, LRU_BLOCKS, LRU_BLOCK)
        pre = jnp.einsum('btnk,znkj->zbtnj', ub, gate_w[d]) + gate_b[d][:, None, None]
        pre = pre.astype(jnp.float32).reshape(2, bsz, t, LRU_WIDTH)
        r, i = jax.nn.sigmoid(pre[0]), jax.nn.sigmoid(pre[1])
        log_a = -LRU_C * r * jax.nn.softplus(-lam[d].astype(jnp.float32))
        return jnp.exp(log_a), jnp.sqrt(-jnp.expm1(2.0 * log_a)) * (i * u.astype(jnp.float32))

    u_c, u_l = x_branch(hc), x_branch(hl)
    hs_c, hs_l = [], []
    for d in range(2):
        a_c, b_c = gates(u_c, d)
        a_l, b_l = gates(u_l, d)
        if d == 1:
            a_c, b_c, a_l, b_l = (jnp.flip(v, axis=1) for v in (a_c, b_c, a_l, b_l))
        h_c = linear_scan(a_c, b_c, jnp.zeros_like(b_c[:, 0]))
        h_l = linear_scan(a_l, b_l, h_c[:, -1])
        if d == 1:
            h_c, h_l = jnp.flip(h_c, axis=1), jnp.flip(h_l, axis=1)
        hs_c.append(h_c)
        hs_l.append(h_l)
    o_l = ((hs_l[0] + hs_l[1]) * gate_branch(hl)).astype(hl.dtype) @ w_out
    o_c = ((hs_c[0] + hs_c[1]) * gate_branch(hc)).astype(hc.dtype) @ w_out if need_ctx else None
    return o_c, o_l


def mla_mixer(hc, hl, need_ctx, cos, sin, w_in, q_norm, kv_norm, w_q_up, w_kv_up, w_out):
    def q_proj(h):
        bsz, t, _ = h.shape
        q = (rmsnorm(h @ w_in[:, :MLA_Q_RANK], q_norm) @ w_q_up).reshape(bsz, t, MLA_HEADS, MLA_QK)
        return q[..., :MLA_NOPE], q[..., MLA_NOPE:]

    def kv_proj(h):
        bsz, t, _ = h.shape
        a = h @ w_in[:, MLA_Q_RANK:]
        kv = (rmsnorm(a[..., :MLA_KV_RANK], kv_norm) @ w_kv_up).reshape(bsz, t, MLA_HEADS, MLA_NOPE + MLA_V)
        return kv[..., :MLA_NOPE], a[..., MLA_KV_RANK:], kv[..., MLA_NOPE:]

    def keys(k_nope, k_rope):
        return jnp.concatenate([k_nope, jnp.broadcast_to(k_rope, k_nope.shape[:3] + (MLA_ROPE,))], axis=-1)

    scale = MLA_QK ** -0.5
    kn_c, kr_c, v_c = kv_proj(hc)
    k_c = keys(kn_c, kr_c[:, :, None])
    kn_l, kr_l, v_l = kv_proj(hl)
    k_l = keys(kn_l, apply_rope(kr_l[:, :, None], cos, sin))
    qn_l, qr_l = q_proj(hl)
    q_plain = jnp.concatenate([qn_l, qr_l], axis=-1)[:, :, :, None]
    q_rot = jnp.concatenate([qn_l, apply_rope(qr_l, cos, sin)], axis=-1)[:, :, :, None]

    def block(args):
        qr, qp = args
        return softmax_attend([(qp, k_c, v_c, None), (qr, k_l, v_l, None)], scale)

    o_l = from_blocks(lax.map(block, (to_blocks(q_rot), to_blocks(q_plain)))) @ w_out
    o_c = None
    if need_ctx:
        qn_c, qr_c = q_proj(hc)
        q_c = jnp.concatenate([qn_c, qr_c], axis=-1)[:, :, :, None]
        o = softmax_attend([(q_c, k_c, v_c, None)], scale)
        o_c = o.reshape(hc.shape[0], hc.shape[1], -1) @ w_out
    return o_c, o_l


def swa_mixer(hc, hl, need_ctx, cos, sin, w_qkv, b_qkv, sink, w_out):
    qd = SWA_Q_HEADS * SWA_HEAD_DIM

    def q_proj(h):
        bsz, t, _ = h.shape
        return (h @ w_qkv[:, :qd] + b_qkv[:qd]).reshape(bsz, t, SWA_KV_HEADS, SWA_GROUP, SWA_HEAD_DIM)

    def kv_proj(h):
        bsz, t, _ = h.shape
        kv = (h @ w_qkv[:, qd:] + b_qkv[qd:]).reshape(bsz, t, 2, SWA_KV_HEADS, SWA_HEAD_DIM)
        return kv[:, :, 0], kv[:, :, 1]

    scale = SWA_HEAD_DIM ** -0.5
    n_lat = hl.shape[1]
    k_c, v_c = kv_proj(hc)
    q_l = q_proj(hl)
    k_l, v_l = kv_proj(hl)
    q_rot = apply_rope(q_l, cos, sin)
    pad = ((0, 0), (ATTN_BLOCK, ATTN_BLOCK), (0, 0), (0, 0))
    kp = jnp.pad(apply_rope(k_l, cos, sin), pad)
    vp = jnp.pad(v_l, pad)

    def block(args):
        b, qr, qp = args
        start = b * ATTN_BLOCK
        kb = lax.dynamic_slice_in_dim(kp, start, 3 * ATTN_BLOCK, axis=1)
        vb = lax.dynamic_slice_in_dim(vp, start, 3 * ATTN_BLOCK, axis=1)
        qpos = start + jnp.arange(ATTN_BLOCK)
        kpos = start - ATTN_BLOCK + jnp.arange(3 * ATTN_BLOCK)
        mask = ((jnp.abs(qpos[:, None] - kpos[None, :]) <= SWA_WINDOW)
                & (kpos >= 0)[None, :] & (kpos < n_lat)[None, :])
        return softmax_attend([(qp, k_c, v_c, None), (qr, kb, vb, mask)], scale, sink)

    nb = n_lat // ATTN_BLOCK
    o_l = from_blocks(lax.map(block, (jnp.arange(nb), to_blocks(q_rot), to_blocks(q_l)))) @ w_out
    o_c = None
    if need_ctx:
        o = softmax_attend([(q_proj(hc), k_c, v_c, None)], scale, sink)
        o_c = o.reshape(hc.shape[0], hc.shape[1], -1) @ w_out
    return o_c, o_l


def conv_ffn(h, w_up, conv_w, conv_b, w_down):
    u = dwconv(h @ w_up, conv_w, conv_b)
    g, v = jnp.split(u, 2, axis=-1)
    return (jax.nn.silu(g) * v) @ w_down


def setup_inputs(seed: int = 0) -> dict:
    key = jax.random.key(seed)
    keys = iter(jax.random.split(key, 64))
    f32 = jnp.float32
    D = D_MODEL

    def nrm(shape, scale):
        return jax.random.normal(next(keys), shape, f32) * scale

    def gain(shape):
        return 1.0 + nrm(shape, 0.05)

    n_ssd = len(range(0, DEPTH, N_MIXERS))
    n_lru = len(range(1, DEPTH, N_MIXERS))
    n_mla = len(range(2, DEPTH, N_MIXERS))
    n_swa = len(range(3, DEPTH, N_MIXERS))

    dt0 = jnp.exp(jax.random.uniform(next(keys), (n_ssd, 2, SSD_N_HEADS), f32, math.log(1e-3), math.log(1e-1)))
    dt_bias = dt0 + jnp.log(-jnp.expm1(-dt0))
    a_log = jnp.log(jax.random.uniform(next(keys), (n_ssd, 2, SSD_N_HEADS), f32, 1.0, 16.0))
    u = jax.random.uniform(next(keys), (n_lru, 2, LRU_WIDTH), f32, 0.9, 0.999)
    s = u ** (1.0 / LRU_C)
    lam = jnp.log(s) - jnp.log1p(-s)

    return {
        "x": nrm((BATCH, SEQ, D), 1.0),
        "c": nrm((BATCH, D), 1.0),
        "ctx": nrm((BATCH, CTX_LEN, D), 1.0),
        "c_ctx": nrm((D,), 1.0),
        "ada_w": nrm((DEPTH, D, ADA_CHUNKS * D), 0.5 * D ** -0.5),
        "ada_b": nrm((DEPTH, ADA_CHUNKS * D), 0.01),
        "norm_mix": gain((DEPTH, D)),
        "norm_ffn": gain((DEPTH, D)),
        "ffn_up": nrm((DEPTH, D, 2 * FFN_HIDDEN), D ** -0.5),
        "ffn_conv_w": nrm((DEPTH, FFN_CONV, 2 * FFN_HIDDEN), FFN_CONV ** -0.5),
        "ffn_conv_b": nrm((DEPTH, 2 * FFN_HIDDEN), 0.01),
        "ffn_down": nrm((DEPTH, FFN_HIDDEN, D), FFN_HIDDEN ** -0.5),
        "final_norm": gain((D,)),
        "ssd_in": nrm((n_ssd, D, SSD_IN_DIM), D ** -0.5),
        "ssd_conv_w": nrm((n_ssd, SSD_CONV, SSD_CONV_DIM), SSD_CONV ** -0.5),
        "ssd_conv_b": nrm((n_ssd, SSD_CONV_DIM), 0.01),
        "ssd_dt_bias": dt_bias,
        "ssd_a_log": a_log,
        "ssd_d": gain((n_ssd, 2, SSD_N_HEADS)),
        "ssd_norm": gain((n_ssd, SSD_D_INNER)),
        "ssd_out": nrm((n_ssd, SSD_D_INNER, D), SSD_D_INNER ** -0.5),
        "lru_in": nrm((n_lru, D, 2 * LRU_WIDTH), D ** -0.5),
        "lru_conv_w": nrm((n_lru, LRU_CONV, LRU_WIDTH), LRU_CONV ** -0.5),
        "lru_conv_b": nrm((n_lru, LRU_WIDTH), 0.01),
        "lru_gate_w": nrm((n_lru, 2, 2, LRU_BLOCKS, LRU_BLOCK, LRU_BLOCK), LRU_BLOCK ** -0.5),
        "lru_gate_b": nrm((n_lru, 2, 2, LRU_BLOCKS, LRU_BLOCK), 0.01),
        "lru_lambda": lam,
        "lru_out": nrm((n_lru, LRU_WIDTH, D), LRU_WIDTH ** -0.5),
        "mla_in": nrm((n_mla, D, MLA_Q_RANK + MLA_KV_RANK + MLA_ROPE), D ** -0.5),
        "mla_q_norm": gain((n_mla, MLA_Q_RANK)),
        "mla_kv_norm": gain((n_mla, MLA_KV_RANK)),
        "mla_q_up": nrm((n_mla, MLA_Q_RANK, MLA_HEADS * MLA_QK), MLA_Q_RANK ** -0.5),
        "mla_kv_up": nrm((n_mla, MLA_KV_RANK, MLA_HEADS * (MLA_NOPE + MLA_V)), MLA_KV_RANK ** -0.5),
        "mla_out": nrm((n_mla, MLA_HEADS * MLA_V, D), (MLA_HEADS * MLA_V) ** -0.5),
        "swa_qkv": nrm((n_swa, D, (SWA_Q_HEADS + 2 * SWA_KV_HEADS) * SWA_HEAD_DIM), D ** -0.5),
        "swa_qkv_b": nrm((n_swa, (SWA_Q_HEADS + 2 * SWA_KV_HEADS) * SWA_HEAD_DIM), 0.01),
        "swa_sink": nrm((n_swa, SWA_Q_HEADS), 1.0),
        "swa_out": nrm((n_swa, SWA_Q_HEADS * SWA_HEAD_DIM, D), (SWA_Q_HEADS * SWA_HEAD_DIM) ** -0.5),
    }


def reference(x, c, ctx, c_ctx, ada_w, ada_b, norm_mix, norm_ffn, ffn_up, ffn_conv_w, ffn_conv_b,
              ffn_down, final_norm, ssd_in, ssd_conv_w, ssd_conv_b, ssd_dt_bias, ssd_a_log, ssd_d,
              ssd_norm, ssd_out, lru_in, lru_conv_w, lru_conv_b, lru_gate_w, lru_gate_b, lru_lambda,
              lru_out, mla_in, mla_q_norm, mla_kv_norm, mla_q_up, mla_kv_up, mla_out, swa_qkv,
              swa_qkv_b, swa_sink, swa_out):
    bsz, n_lat, _ = x.shape
    cos, sin = axial_rope(n_lat)
    sc = jax.nn.silu(c)
    scc = jax.nn.silu(c_ctx)
    xl, xc = x, ctx
    for i in range(DEPTH):
        kind, j = i % N_MIXERS, i // N_MIXERS
        need_ctx = i < DEPTH - 1
        mod_l = (sc @ ada_w[i] + ada_b[i]).reshape(bsz, ADA_CHUNKS, 1, D_MODEL)
        mod_c = (scc @ ada_w[i] + ada_b[i]).reshape(ADA_CHUNKS, 1, 1, D_MODEL)
        hl = modulate(rmsnorm(xl, norm_mix[i]), mod_l[:, 0], mod_l[:, 1])
        hc = modulate(rmsnorm(xc, norm_mix[i]), mod_c[0], mod_c[1])
        if kind == 0:
            oc, ol = ssd_mixer(hc, hl, need_ctx, ssd_in[j], ssd_conv_w[j], ssd_conv_b[j], ssd_dt_bias[j],
                               ssd_a_log[j], ssd_d[j], ssd_norm[j], ssd_out[j])
        elif kind == 1:
            oc, ol = rglru_mixer(hc, hl, need_ctx, lru_in[j], lru_conv_w[j], lru_conv_b[j], lru_gate_w[j],
                                 lru_gate_b[j], lru_lambda[j], lru_out[j])
        elif kind == 2:
            oc, ol = mla_mixer(hc, hl, need_ctx, cos, sin, mla_in[j], mla_q_norm[j], mla_kv_norm[j],
                               mla_q_up[j], mla_kv_up[j], mla_out[j])
        else:
            oc, ol = swa_mixer(hc, hl, need_ctx, cos, sin, swa_qkv[j], swa_qkv_b[j], swa_sink[j], swa_out[j])
        xl = (xl + mod_l[:, 2] * ol).astype(x.dtype)
        hl = modulate(rmsnorm(xl, norm_ffn[i]), mod_l[:, 3], mod_l[:, 4])
        xl = (xl + mod_l[:, 5] * conv_ffn(hl, ffn_up[i], ffn_conv_w[i], ffn_conv_b[i], ffn_down[i])).astype(x.dtype)
        if need_ctx:
            xc = (xc + mod_c[2] * oc).astype(ctx.dtype)
            hc = modulate(rmsnorm(xc, norm_ffn[i]), mod_c[3], mod_c[4])
            xc = (xc + mod_c[5] * conv_ffn(hc, ffn_up[i], ffn_conv_w[i], ffn_conv_b[i], ffn_down[i])).astype(ctx.dtype)
    return rmsnorm(xl, final_norm)
```

```python
import numpy as np
import ml_dtypes
import concourse.bass as bass
import concourse.mybir as mybir
from concourse.bass_utils import run_bass_kernel_spmd

F32 = mybir.dt.float32
BF16 = mybir.dt.bfloat16
AF = mybir.ActivationFunctionType
ALU = mybir.AluOpType
AX = mybir.AxisListType
NPBF = ml_dtypes.bfloat16

NCORES = 8
D = 2048
KC = D // 128
SEQ = 8192
CTX = 256
EPS = 1e-6


class Prog:
    ENG = ("pe", "act", "dve", "pool", "sp")

    def __init__(self):
        self.nc = bass.Bass("TRN2", target_bir_lowering=False)
        self.ops = {e: [] for e in self.ENG}
        self.cnt = {e: 0 for e in self.ENG}
        self.dma_cnt = {e: 0 for e in self.ENG}
        self.NDS = 16
        self.seen = {e: {} for e in self.ENG}
        self.acc = {}
        self.free_sz = {}
        self.readonly = set()
        self.psum_names = set()
        self.tensors = []
        self.sems = {}
        self.final_dma = {}
        self._stack = []

    def dram(self, name, shape, dtype, kind):
        t = self.nc.dram_tensor(name, list(shape), dtype, kind=kind)
        if kind == "ExternalInput":
            self.readonly.add(name)
        self.free_sz[name] = None
        return t.ap()

    def sbuf(self, name, shape, dtype):
        cm = self.nc.sbuf_tensor(name, list(shape), dtype)
        t = cm.__enter__()
        self._stack.append(cm)
        f = np.dtype(mybir.dt.np(dtype)).itemsize
        for s in shape[1:]:
            f *= s
        self.free_sz[name] = f
        return t

    def psum(self, name, shape, dtype=F32):
        cm = self.nc.psum_tensor(name, list(shape), dtype)
        self.psum_names.add(name)
        t = cm.__enter__()
        self._stack.append(cm)
        f = np.dtype(mybir.dt.np(dtype)).itemsize
        for s in shape[1:]:
            f *= s
        self.free_sz[name] = f
        return t

    def _box(self, ap):
        name = ap.tensor.name
        F = self.free_sz.get(name)
        off = int(ap.offset)
        if F is None:
            lo = hi = off
            for st, n in ap.ap:
                if n > 1:
                    if st >= 0:
                        hi += st * (n - 1)
                    else:
                        lo += st * (n - 1)
            return name, (0, 0, lo, hi)
        es = np.dtype(mybir.dt.np(ap.dtype)).itemsize
        F = F // es
        p0 = off // F
        f0 = off % F
        plo = phi = p0
        flo = fhi = f0
        for st, n in ap.ap:
            if n <= 1 or st == 0:
                continue
            if st % F == 0:
                d = (st // F) * (n - 1)
                if d >= 0:
                    phi += d
                else:
                    plo += d
            else:
                d = st * (n - 1)
                if d >= 0:
                    fhi += d
                else:
                    flo += d
        b0, b1 = flo * es, fhi * es + es - 1
        if name in self.psum_names:
            b0, b1 = (b0 // 2048) * 2048, (b1 // 2048) * 2048 + 2047
        return name, (plo, phi, b0, b1)

    @staticmethod
    def _ovl(a, b):
        return not (a[1] < b[0] or b[1] < a[0] or a[3] < b[2] or b[3] < a[2])

    @staticmethod
    def _contains(a, b):
        return a[0] <= b[0] and a[1] >= b[1] and a[2] <= b[2] and a[3] >= b[3]

    def _deps(self, eng, reads, writes, event):
        waits = {}

        def need(ev):
            k, v = ev
            if waits.get(k, 0) < v:
                waits[k] = v

        for ap in reads:
            name, box = self._box(ap)
            if name in self.readonly:
                continue
            lst = self.acc.setdefault(name, [])
            for (b, ev, isw) in lst:
                if isw and self._ovl(b, box):
                    need(ev)
            lst.append((box, event, False))
        for ap in writes:
            name, box = self._box(ap)
            lst = self.acc.setdefault(name, [])
            keep = []
            for rec in lst:
                b, ev, isw = rec
                if self._ovl(b, box):
                    if ev != event:
                        need(ev)
                    if self._contains(box, b):
                        continue
                keep.append(rec)
            keep.append((box, event, True))
            self.acc[name] = keep
        out = []
        seen = self.seen[eng]
        for k, v in waits.items():
            if k == ("c", eng) and eng == "pe":
                continue
            if seen.get(k, 0) >= v:
                continue
            seen[k] = v
            out.append((k, v))
        return out

    def op(self, eng, fn, reads=(), writes=()):
        self.cnt[eng] += 1
        ev = (("c", eng), self.cnt[eng])
        waits = self._deps(eng, reads, writes, ev)
        self.ops[eng].append((fn, waits, (("c", eng), 1)))

    def dma(self, q, out, in_, **kw):
        i = self.dma_cnt[q]
        self.dma_cnt[q] += 1
        key = ("d", q, i % self.NDS)
        val = 16 * (i // self.NDS + 1)
        ev = (key, val)
        self.final_dma[key] = val
        waits = self._deps(q, [in_], [out], ev)
        if val > 16 and self.seen[q].get(key, 0) < val - 16:
            self.seen[q][key] = val - 16
            waits.append((key, val - 16))
        self.ops[q].append((lambda e: e.dma_start(out=out, in_=in_, **kw), waits, (key, 16)))

    def emit(self):
        nc = self.nc
        keys = set()
        for e in self.ENG:
            for fn, waits, inc in self.ops[e]:
                keys.add(inc[0])
                for k, v in waits:
                    keys.add(k)
        sem = {}
        for k in sorted(keys, key=str):
            cm = nc.semaphore("s_" + "_".join(str(x) for x in k))
            sem[k] = cm.__enter__()
            self._stack.append(cm)
        final = [(k, v) for k, v in self.final_dma.items()]
        ops = self.ops
        with nc.Block() as block:
            def run(e, engobj):
                for fn, waits, inc in ops[e]:
                    for k, v in waits:
                        engobj.wait_ge(sem[k], v)
                    ins = fn(engobj)
                    ins.then_inc(sem[inc[0]], inc[1])

            @block.tensor
            def _(t):
                run("pe", t)

            @block.scalar
            def _(s):
                run("act", s)

            @block.vector
            def _(v):
                run("dve", v)

            @block.gpsimd
            def _(g):
                run("pool", g)

            @block.sync
            def _(s):
                run("sp", s)
                for k, v in final:
                    s.wait_ge(sem[k], v)
        for cm in reversed(self._stack):
            cm.__exit__(None, None, None)
        self._stack = []
        return nc

    def mm(self, out, lhsT, rhs, start=True, stop=True):
        self.op("pe", lambda e: e.matmul(out, lhsT, rhs, start=start, stop=stop),
                reads=[lhsT, rhs], writes=[out])

    def act(self, out, in_, func, bias=None, scale=None, accum_out=None, eng="act"):
        kw = {}
        reads = [in_]
        if bias is not None:
            kw["bias"] = bias
            if not isinstance(bias, (int, float)):
                reads.append(bias)
        if scale is not None:
            kw["scale"] = scale
            if not isinstance(scale, (int, float)):
                reads.append(scale)
        writes = [out]
        if accum_out is not None:
            kw["accum_out"] = accum_out
            writes.append(accum_out)
        self.op("act", lambda e: e.activation(out, in_, func, **kw), reads=reads, writes=writes)

    def tt(self, out, in0, in1, op, eng="dve"):
        self.op(eng, lambda e: e.tensor_tensor(out, in0, in1, op), reads=[in0, in1], writes=[out])

    def ts(self, out, in0, s1, s2, op0, op1=None, eng="dve"):
        reads = [in0] + [s for s in (s1, s2) if s is not None and not isinstance(s, (int, float))]
        if op1 is None:
            self.op(eng, lambda e: e.tensor_scalar(out, in0, s1, None, op0), reads=reads, writes=[out])
        else:
            self.op(eng, lambda e: e.tensor_scalar(out, in0, s1, s2, op0, op1), reads=reads, writes=[out])

    def stt(self, out, in0, scalar, in1, op0, op1):
        reads = [in0, in1] + ([scalar] if not isinstance(scalar, (int, float)) else [])
        self.op("dve", lambda e: e.scalar_tensor_tensor(out, in0, scalar, in1, op0, op1),
                reads=reads, writes=[out])

    def copy(self, out, in_, eng="dve"):
        if eng == "act":
            self.op("act", lambda e: e.copy(out, in_), reads=[in_], writes=[out])
        else:
            self.op(eng, lambda e: e.tensor_copy(out, in_), reads=[in_], writes=[out])

    def memset(self, ap, val, eng="dve"):
        self.op(eng, lambda e: e.memset(ap, val), reads=[], writes=[ap])

    def recip(self, out, in_):
        self.op("dve", lambda e: e.reciprocal(out, in_), reads=[in_], writes=[out])

    def scan(self, out, d0, d1, initial, op0=ALU.mult, op1=ALU.add):
        reads = [d0, d1] + ([initial] if not isinstance(initial, (int, float)) else [])
        self.op("dve", lambda e: e.tensor_tensor_scan(out, d0, d1, initial, op0, op1),
                reads=reads, writes=[out])

    def transpose(self, out, in_, ident):
        self.op("pe", lambda e: e.transpose(out, in_, ident), reads=[in_, ident], writes=[out])


def run_prog(p, in_maps):
    nc = p.emit()
    res = run_bass_kernel_spmd(nc, in_maps, core_ids=list(range(NCORES)))
    return res.results


MODC = 6 * D // NCORES


def build_mods():
    p = Prog()
    cT = p.dram("cT", [128, KC, 2], F32, "ExternalInput")
    w = p.dram("w", [4, D, MODC], F32, "ExternalInput")
    b = p.dram("b", [4, MODC], F32, "ExternalInput")
    out = p.dram("mods", [2, 4, MODC], F32, "ExternalOutput")
    s = p.sbuf("s", [128, KC, 2], F32)
    sg = p.sbuf("sg", [128, KC, 2], F32)
    bt = p.sbuf("bt", [2, 4, MODC], F32)
    ot = p.sbuf("ot", [2, 4, MODC], F32)
    wb = [p.sbuf(f"wb{i}", [128, KC, 512], F32) for i in range(2)]
    ps = p.psum("ps", [128, 8, 512], F32)
    p.dma("sp", s[:], cT)
    for r in range(2):
        p.dma("sp", bt[r:r + 1], b[None, :, :])
    p.act(sg[:], s[:], AF.Sigmoid)
    p.tt(s[:], s[:], sg[:], ALU.mult)
    n = 0
    for i in range(4):
        wv = w[i].rearrange("(k p) n -> p k n", p=128)
        for cb in range(MODC // 512):
            buf = wb[n % 2]
            p.dma("sp" if n % 2 == 0 else "act", buf[:], wv[:, :, cb * 512:(cb + 1) * 512])
            acc = ps[0:2, n % 2, :]
            for k in range(KC):
                p.mm(acc, s[:, k, :], buf[:, k, :], start=(k == 0), stop=(k == KC - 1))
            p.tt(ot[:, i, cb * 512:(cb + 1) * 512], acc, bt[:, i, cb * 512:(cb + 1) * 512], ALU.add)
            n += 1
    p.dma("sp", out, ot[:])
    return p


def run_mods(inp):
    c2 = np.stack([inp["c"][0], inp["c_ctx"]], axis=-1)
    cT = np.ascontiguousarray(c2.reshape(KC, 128, 2).transpose(1, 0, 2))
    in_maps = []
    for c in range(NCORES):
        sl = slice(c * MODC, (c + 1) * MODC)
        in_maps.append({"cT": cT, "w": np.ascontiguousarray(inp["ada_w"][:, :, sl]),
                        "b": np.ascontiguousarray(inp["ada_b"][:, sl])})
    res = run_prog(build_mods(), in_maps)
    mods = np.concatenate([r["mods"] for r in res], axis=-1)
    return mods.reshape(2, 4, 6, D)


NL = SEQ // NCORES
NCX = CTX // NCORES
NT = NL + 2 + NCX + 2
LAT0, CTX0 = 1, NL + 3
BLOCKS = ((0, 512), (512, 1024), (1024, NT))
FFN_H = 5632
HC = FFN_H // 128
GRP = 2
VP_ML, VP_MC, VP_NF, VP_NL, VP_NC, VP_NM = 0, 6, 12, 13, 19, 25
NVP = 26


def col_ranges(c0, c1):
    out = []
    if c0 < NL + 2:
        out.append((c0, min(c1, NL + 2), 0))
    if c1 > NL + 2:
        out.append((max(c0, NL + 2), c1, 1))
    return out


def emit_norm(p, ps, xT, hT, tmp, rstd, sqb, ones, A, B, ncols=NT, out_f32=None):
    blocks = [b for b in BLOCKS if b[0] < ncols]
    for bi, (c0, c1) in enumerate(blocks):
        c1 = min(c1, ncols)
        acc = ps[:, bi, 0:c1 - c0]
        for k in range(KC):
            p.act(sqb[:, 0:c1 - c0], xT[:, k, c0:c1], AF.Square) if k % 2 == 0 else \
                p.tt(sqb[:, 512:512 + c1 - c0], xT[:, k, c0:c1], xT[:, k, c0:c1], ALU.mult)
            src = sqb[:, 0:c1 - c0] if k % 2 == 0 else sqb[:, 512:512 + c1 - c0]
            p.mm(acc, ones[:], src, start=(k == 0), stop=(k == KC - 1))
        p.ts(rstd[:, c0:c1], acc, 1.0 / D, EPS, ALU.mult, ALU.add)
    p.act(rstd[:, 0:ncols], rstd[:, 0:ncols], AF.Sqrt)
    p.recip(rstd[:, 0:ncols], rstd[:, 0:ncols])
    for k in range(KC):
        p.tt(tmp[:, 0:ncols], xT[:, k, 0:ncols], rstd[:, 0:ncols], ALU.mult)
        for (a0, a1, side) in col_ranges(0, ncols):
            dst = (out_f32 if out_f32 is not None else hT)[:, k, a0:a1]
            p.act(dst, tmp[:, a0:a1], AF.Identity, scale=A[side][:, k:k + 1],
                  bias=B[side][:, k:k + 1])


def build_T(prev_fc, nxt):
    p = Prog()
    xin = p.dram("xT", [128, KC, NT], F32, "ExternalInput")
    vp = p.dram("vp", [128, NVP, KC], F32, "ExternalInput")
    cmask = p.dram("cmask", [128, 4], F32, "ExternalInput")
    xT = p.sbuf("x", [128, KC, NT], F32)
    hb = p.sbuf("hb", [128, KC * NT], BF16)
    hT = hb[:, :].rearrange("p (k n) -> p k n", n=NT)
    v = p.sbuf("v", [128, NVP, KC], F32)
    cm = p.sbuf("cm", [128, 4], F32)
    ones = p.sbuf("ones", [128, 128], BF16)
    tmp = p.sbuf("tmp", [128, NT], F32)
    rstd = p.sbuf("rstd", [128, NT], F32)
    sqb = p.sbuf("sqb", [128, 1024], BF16)
    Aff = p.sbuf("Aff", [128, 8, KC], F32)
    ps = p.psum("ps", [128, 8, 512], F32)
    p.dma("sp", xT[:], xin)
    p.dma("sp", v[:], vp)
    p.dma("sp", cm[:], cmask)
    p.memset(ones[:], 1.0)

    def affine(slot, gi, si, hi, side_base):
        for side in range(2):
            mb = (VP_ML, VP_MC)[side] if side_base == 0 else (VP_NL, VP_NC)[side]
            p.ts(Aff[:, slot + side, :], v[:, mb + si, :], 1.0, None, ALU.add)
            p.tt(Aff[:, slot + side, :], Aff[:, slot + side, :], v[:, gi, :], ALU.mult)
        A = [Aff[:, slot + s, :] for s in range(2)]
        mbs = (VP_ML, VP_MC) if side_base == 0 else (VP_NL, VP_NC)
        B = [v[:, mbs[s] + hi, :] for s in range(2)]
        return A, B

    def zero_halo(buf):
        for j, col in enumerate((0, NL + 1, NL + 2, NT - 1)):
            p.ts(buf[:, :, col:col + 1], buf[:, :, col:col + 1], cm[:, j:j + 1], None, ALU.mult)

    if prev_fc is not None:
        FC = prev_fc
        yin = p.dram("yT", [128, FC, NT], BF16, "ExternalInput")
        wout = p.dram("w_out", [FC * 128, D], F32, "ExternalInput")
        wup = p.dram("w_up", [D, 2 * FFN_H], F32, "ExternalInput")
        wdn = p.dram("w_dn", [FFN_H, D], F32, "ExternalInput")
        cw_in = p.dram("cw", [128, 2 * HC, 3], F32, "ExternalInput")
        cb_in = p.dram("cb", [128, 2 * HC], F32, "ExternalInput")
        cw = p.sbuf("cwt", [128, 2 * HC, 3], F32)
        cb = p.sbuf("cbt", [128, 2 * HC], F32)
        p.dma("sp", cw[:], cw_in)
        p.dma("sp", cb[:], cb_in)
        wub = [p.sbuf(f"wu{i}", [128, KC * 512], BF16) for i in range(2)]
        wdb = [p.sbuf(f"wd{i}", [128, GRP, D], BF16) for i in range(2)]
        actb = [p.sbuf(f"act{i}", [128, GRP, NT], BF16) for i in range(2)]
        upre = [p.sbuf(f"upre{i}", [128, NT], F32) for i in range(2)]
        cg = p.sbuf("cg", [128, NT], F32)
        cv = p.sbuf("cv", [128, NT], F32)
        sg = p.sbuf("sgt", [128, NT], F32)
        for a in actb:
            p.memset(a[:], 0.0)
        wov = wout.rearrange("(c q) n -> q c n", q=128)
        nw = 0
        for bi, (c0, c1) in enumerate(BLOCKS):
            n = c1 - c0
            yblk = hb[:, 0:FC * n].rearrange("p (k n) -> p k n", n=n)
            p.dma("sp", yblk, yin[:, :, c0:c1])
            for m in range(KC):
                wb = wub[nw % 2][:, 0:FC * 128].rearrange("p (k n) -> p k n", n=128)
                p.dma("pool", wb, wov[:, :, m * 128:(m + 1) * 128])
                acc = ps[:, 6 + nw % 2, 0:n]
                for fc in range(FC):
                    p.mm(acc, wb[:, fc, :], yblk[:, fc, :], start=(fc == 0), stop=(fc == FC - 1))
                for (a0, a1, side) in col_ranges(c0, c1):
                    g = v[:, (VP_ML, VP_MC)[side] + 2, m:m + 1]
                    p.stt(xT[:, m, a0:a1], acc[:, a0 - c0:a1 - c0], g, xT[:, m, a0:a1], ALU.mult, ALU.add)
                nw += 1
        A, B = affine(0, VP_NF, 4, 3, 0)
        emit_norm(p, ps, xT, hT, tmp, rstd, sqb, ones, A, B)
        zero_halo(hT)
        wuv = wup.rearrange("(k q) n -> q k n", q=128)
        wdv = wdn.rearrange("(j q) n -> q j n", q=128)
        NG = HC // GRP
        for g in range(NG):
            wu = wub[g % 2][:, :].rearrange("p (k n) -> p k n", n=512)
            gw = GRP * 128
            p.dma("pool", wu[:, :, 0:gw], wuv[:, :, g * gw:(g + 1) * gw])
            p.dma("pool", wu[:, :, gw:2 * gw], wuv[:, :, FFN_H + g * gw:FFN_H + (g + 1) * gw])
            wd = wdb[g % 2]
            p.dma("pool", wd[:], wdv[:, g * GRP:(g + 1) * GRP, :])
            act = actb[g % 2]
            no = 0
            for jj in range(GRP):
                for half in range(2):
                    oc = half * GRP + jj
                    ch = half * HC + g * GRP + jj
                    pbase = 3 * (no % 2)
                    up = upre[no % 2]
                    for bi, (c0, c1) in enumerate(BLOCKS):
                        acc = ps[:, pbase + bi, 0:c1 - c0]
                        for k in range(KC):
                            p.mm(acc, wu[:, k, oc * 128:(oc + 1) * 128], hT[:, k, c0:c1],
                                 start=(k == 0), stop=(k == KC - 1))
                        p.copy(up[:, c0:c1], acc, eng="act")
                    dst = cg if half == 0 else cv
                    p.act(dst[:, 1:NT - 1], up[:, 1:NT - 1], AF.Identity,
                          scale=cw[:, ch, 1:2], bias=cb[:, ch:ch + 1])
                    p.stt(dst[:, 1:NT - 1], up[:, 0:NT - 2], cw[:, ch, 0:1], dst[:, 1:NT - 1], ALU.mult, ALU.add)
                    p.stt(dst[:, 1:NT - 1], up[:, 2:NT], cw[:, ch, 2:3], dst[:, 1:NT - 1], ALU.mult, ALU.add)
                    no += 1
                p.act(sg[:, 1:NT - 1], cg[:, 1:NT - 1], AF.Silu)
                p.tt(act[:, jj, 1:NT - 1], cv[:, 1:NT - 1], sg[:, 1:NT - 1], ALU.mult)
            nd = 0
            for m in range(KC):
                for bi, (c0, c1) in enumerate(BLOCKS):
                    acc = ps[:, 6 + nd % 2, 0:c1 - c0]
                    for jj in range(GRP):
                        p.mm(acc, wd[:, jj, m * 128:(m + 1) * 128], act[:, jj, c0:c1],
                             start=(jj == 0), stop=(jj == GRP - 1))
                    for (a0, a1, side) in col_ranges(c0, c1):
                        gt = v[:, (VP_ML, VP_MC)[side] + 5, m:m + 1]
                        p.stt(xT[:, m, a0:a1], acc[:, a0 - c0:a1 - c0], gt, xT[:, m, a0:a1], ALU.mult, ALU.add)
                    nd += 1
        xo = p.dram("x_out", [128, KC, NL + NCX], F32, "ExternalOutput")
        p.dma("sp", xo[:, :, 0:NL], xT[:, :, LAT0:LAT0 + NL])
        p.dma("sp", xo[:, :, NL:NL + NCX], xT[:, :, CTX0:CTX0 + NCX])
    if nxt == "final":
        fo = p.dram("f_out", [128, KC, NL], F32, "ExternalOutput")
        A = [v[:, VP_NM, :], v[:, VP_NM, :]]
        emit_norm_final(p, ps, xT, tmp, rstd, sqb, ones, A, fo)
    else:
        A, B = affine(2, VP_NM, 1, 0, 1)
        emit_norm(p, ps, xT, hT, tmp, rstd, sqb, ones, A, B)
        ho = p.dram("h_out", [128, KC, NL + NCX], BF16, "ExternalOutput")
        p.dma("sp", ho[:, :, 0:NL], hT[:, :, LAT0:LAT0 + NL])
        p.dma("sp", ho[:, :, NL:NL + NCX], hT[:, :, CTX0:CTX0 + NCX])
    return p


def emit_norm_final(p, ps, xT, tmp, rstd, sqb, ones, A, fo):
    ncols = NL + 2
    for bi, (c0, c1) in enumerate(BLOCKS):
        c1 = min(c1, ncols)
        acc = ps[:, bi, 0:c1 - c0]
        for k in range(KC):
            p.act(sqb[:, 0:c1 - c0], xT[:, k, c0:c1], AF.Square)
            p.mm(acc, ones[:], sqb[:, 0:c1 - c0], start=(k == 0), stop=(k == KC - 1))
        p.ts(rstd[:, c0:c1], acc, 1.0 / D, EPS, ALU.mult, ALU.add)
    p.act(rstd[:, 0:ncols], rstd[:, 0:ncols], AF.Sqrt)
    p.recip(rstd[:, 0:ncols], rstd[:, 0:ncols])
    for k in range(KC):
        p.tt(tmp[:, 0:ncols], xT[:, k, 0:ncols], rstd[:, 0:ncols], ALU.mult)
        p.ts(xT[:, k, 0:ncols], tmp[:, 0:ncols], A[0][:, k:k + 1], None, ALU.mult)
    p.dma("sp", fo, xT[:, :, LAT0:LAT0 + NL])


def fm(a):
    T, F = a.shape
    return np.ascontiguousarray(a.reshape(T, F // 128, 128).transpose(2, 1, 0))


def tm(a):
    P, FC, T = a.shape
    return np.ascontiguousarray(a.transpose(2, 1, 0).reshape(T, FC * 128))


def core_cols(al, ac, c):
    F = al.shape[1]
    buf = np.zeros((NT, F), al.dtype)
    lo, hi = c * NL - 1, c * NL + NL + 1
    s0, s1 = max(lo, 0), min(hi, SEQ)
    buf[s0 - lo:s0 - lo + (s1 - s0)] = al[s0:s1]
    lo, hi = c * NCX - 1, c * NCX + NCX + 1
    s0, s1 = max(lo, 0), min(hi, CTX)
    buf[NL + 2 + s0 - lo:NL + 2 + s0 - lo + (s1 - s0)] = ac[s0:s1]
    return fm(buf)


def vecfm(vv):
    return np.ascontiguousarray(vv.reshape(KC, 128).T)


def make_vp(mods, inp, prev, nxt_layer, final=False):
    vp = np.zeros((128, NVP, KC), np.float32)
    if prev is not None:
        for j in range(6):
            vp[:, VP_ML + j] = vecfm(mods[0, prev, j])
            vp[:, VP_MC + j] = vecfm(mods[1, prev, j])
        vp[:, VP_NF] = vecfm(inp["norm_ffn"][prev])
    if final:
        vp[:, VP_NM] = vecfm(inp["final_norm"])
    else:
        for j in range(6):
            vp[:, VP_NL + j] = vecfm(mods[0, nxt_layer, j])
            vp[:, VP_NC + j] = vecfm(mods[1, nxt_layer, j])
        vp[:, VP_NM] = vecfm(inp["norm_mix"][nxt_layer])
    return vp


def cmask_for(c):
    m = np.ones((128, 4), np.float32)
    if c == 0:
        m[:, 0] = 0
        m[:, 2] = 0
    if c == NCORES - 1:
        m[:, 1] = 0
        m[:, 3] = 0
    return m


_T_CACHE = {}


def run_T(inp, mods, xl, xc, prev, yl, yc, w_out, nxt):
    final = nxt == "final"
    nxt_layer = None if final else (0 if prev is None else prev + 1)
    fc = None if prev is None else yl.shape[1] // 128
    vp = make_vp(mods, inp, prev, nxt_layer, final)
    in_maps = []
    for c in range(NCORES):
        m = {"xT": core_cols(xl, xc, c), "vp": vp, "cmask": cmask_for(c)}
        if prev is not None:
            m["yT"] = core_cols(yl, yc, c)
            m["w_out"] = w_out
            m["w_up"] = inp["ffn_up"][prev]
            m["w_dn"] = inp["ffn_down"][prev]
            cwv = inp["ffn_conv_w"][prev]
            m["cw"] = np.ascontiguousarray(cwv.reshape(3, 2 * HC, 128).transpose(2, 1, 0))
            m["cb"] = np.ascontiguousarray(inp["ffn_conv_b"][prev].reshape(2 * HC, 128).T)
        in_maps.append(m)
    res = run_prog(build_T(fc, "final" if final else "h"), in_maps)
    out = {}
    if prev is not None:
        xo = [tm(r["x_out"]) for r in res]
        out["xl"] = np.concatenate([a[:NL] for a in xo], 0)
        out["xc"] = np.concatenate([a[NL:] for a in xo], 0)
    if final:
        out["f"] = np.concatenate([tm(r["f_out"]) for r in res], 0)
    else:
        ho = [tm(r["h_out"]) for r in res]
        out["hl"] = np.concatenate([a[:NL] for a in ho], 0)
        out["hc"] = np.concatenate([a[NL:] for a in ho], 0)
    return out


TOK = CTX + SEQ
TPAD = TOK + 6
SEGS = ((1, 0, CTX), (CTX + 4, CTX, SEQ))


def tok_blocks(bs=512):
    out = []
    for (pc, u0, n) in SEGS:
        for o in range(0, n, bs):
            out.append((pc + o, u0 + o, min(bs, n - o)))
    return out


def pad_tokens(hc, hl):
    F = hc.shape[1]
    buf = np.zeros((TPAD, F), hc.dtype)
    buf[1:1 + CTX] = hc
    buf[CTX + 4:CTX + 4 + SEQ] = hl
    return buf


def emit_conv4(p, pre, dst_f32, cwt, cbt, ch, n, tmpc):
    p.act(tmpc[:, 0:n], pre[:, 1:1 + n], AF.Identity, scale=cwt[:, ch, 1:2], bias=cbt[:, ch:ch + 1])
    p.stt(tmpc[:, 0:n], pre[:, 0:n], cwt[:, ch, 0:1], tmpc[:, 0:n], ALU.mult, ALU.add)
    p.stt(tmpc[:, 0:n], pre[:, 2:2 + n], cwt[:, ch, 2:3], tmpc[:, 0:n], ALU.mult, ALU.add)
    p.stt(dst_f32, pre[:, 3:3 + n], cwt[:, ch, 3:4], tmpc[:, 0:n], ALU.mult, ALU.add)


def build_lru():
    p = Prog()
    hin = p.dram("hT", [128, KC, TPAD], BF16, "ExternalInput")
    win = p.dram("w_in", [D, 512], F32, "ExternalInput")
    cw_in = p.dram("cw", [128, 2, 4], F32, "ExternalInput")
    cb_in = p.dram("cb", [128, 2], F32, "ExternalInput")
    gw_in = p.dram("gw", [4, 256, 256], F32, "ExternalInput")
    gb_in = p.dram("gb", [128, 4, 2], F32, "ExternalInput")
    lam_in = p.dram("lam", [128, 2, 2], F32, "ExternalInput")
    out = p.dram("oT", [128, 2, TOK], BF16, "ExternalOutput")

    u = p.sbuf("u", [128, 2, TOK], F32)
    ub = p.sbuf("ub", [128, 2, TOK], BF16)
    gl = p.sbuf("gl", [128, 2, TOK], BF16)
    BS = 1024
    ARENA = max(KC * 512 + 2 * KC * 515, 2 * TOK + 2 * 7 * BS)
    ar = p.sbuf("arena", [128, ARENA], BF16)
    wsb = ar[:, 0:KC * 512].rearrange("p (k n) -> p k n", n=512)
    hblk = [ar[:, KC * 512 + i * KC * 515:KC * 512 + (i + 1) * KC * 515].rearrange("p (k n) -> p k n", n=515)
            for i in range(2)]
    arf = ar[:, :].bitcast(F32)
    hs = arf[:, 0:TOK]
    tmps = [arf[:, TOK + i * BS:TOK + (i + 1) * BS] for i in range(7)]
    cwt = p.sbuf("cwt", [128, 2, 4], F32)
    cbt = p.sbuf("cbt", [128, 2], F32)
    gw = p.sbuf("gwt", [128, 4, 2, 256], BF16)
    gb = p.sbuf("gbt", [128, 4, 2], F32)
    lam = p.sbuf("lamt", [128, 2, 2], F32)
    nsp = p.sbuf("nsp", [128, 2, 2], F32)
    nsp2 = p.sbuf("nsp2", [128, 2, 2], F32)
    pre = p.sbuf("pre", [128, 520], F32)
    tmpc = p.sbuf("tmpc", [128, 512], F32)
    g1 = p.sbuf("g1", [128, 512], F32)
    g2 = p.sbuf("g2", [128, 512], F32)
    ps = p.psum("ps", [128, 8, 512], F32)

    p.dma("pool", wsb, win.rearrange("(k q) n -> q k n", q=128))
    p.dma("sp", cwt[:], cw_in)
    p.dma("sp", cbt[:], cb_in)
    p.dma("pool", gw[:], gw_in.rearrange("z (k q) n -> q z k n", q=128))
    p.dma("sp", gb[:], gb_in)
    p.dma("sp", lam[:], lam_in)
    p.act(nsp[:], lam[:], AF.Exp, scale=-1.0)
    p.act(nsp[:], nsp[:], AF.Ln, bias=1.0)
    p.ts(nsp2[:], nsp[:], -16.0, None, ALU.mult)
    p.ts(nsp[:], nsp[:], -8.0, None, ALU.mult)

    for bi, (pc, u0, n) in enumerate(tok_blocks(512)):
        hb = hblk[bi % 2]
        ncol = n + 3
        p.dma("sp", hb[:, :, 0:ncol], hin[:, :, pc - 1:pc - 1 + ncol])
        for m in range(2):
            pieces = [(0, min(512, ncol))] + ([(512, ncol)] if ncol > 512 else [])
            for pi, (c0, c1) in enumerate(pieces):
                acc = ps[:, (2 * m + pi) % 8, 0:c1 - c0]
                for k in range(KC):
                    p.mm(acc, wsb[:, k, m * 128:(m + 1) * 128], hb[:, k, c0:c1], start=(k == 0), stop=(k == KC - 1))
                p.copy(pre[:, c0:c1], acc, eng="act")
            emit_conv4(p, pre, u[:, m, u0:u0 + n], cwt, cbt, m, n, tmpc)
            p.copy(ub[:, m, u0:u0 + n], u[:, m, u0:u0 + n], eng="pool")
            acc = ps[:, 4 + m, 0:n]
            for k in range(KC):
                p.mm(acc, wsb[:, k, 256 + m * 128:256 + (m + 1) * 128], hb[:, k, 1:1 + n], start=(k == 0), stop=(k == KC - 1))
            p.act(g1[:, 0:n], acc, AF.Square)
            p.ts(g1[:, 0:n], g1[:, 0:n], 0.044715, 1.0, ALU.mult, ALU.add)
            p.tt(g1[:, 0:n], g1[:, 0:n], acc, ALU.mult)
            p.act(g2[:, 0:n], g1[:, 0:n], AF.Sigmoid, scale=1.5957691216057308)
            p.tt(gl[:, m, u0:u0 + n], g2[:, 0:n], acc, ALU.mult)

    blocks = tok_blocks(BS)
    for m in range(2):
        for d in range(2):
            order = blocks if d == 0 else [blocks[0]] + blocks[:0:-1]
            prev = None
            for (pc, u0, n) in order:
                r, ig, a, sq, b, t, hcur = [x[:, 0:n] for x in tmps]
                for z in range(2):
                    for h0 in range(0, n, 512):
                        h1 = min(n, h0 + 512)
                        acc = ps[:, (2 * z + h0 // 512) % 8, 0:h1 - h0]
                        for k in range(2):
                            p.mm(acc, gw[:, 2 * d + z, k, m * 128:(m + 1) * 128], ub[:, k, u0 + h0:u0 + h1],
                                 start=(k == 0), stop=(k == 1))
                        p.act((r if z == 0 else ig)[:, h0:h1], acc, AF.Sigmoid, bias=gb[:, 2 * d + z, m:m + 1])
                p.act(a, r, AF.Exp, scale=nsp[:, d, m:m + 1])
                p.act(sq, r, AF.Exp, scale=nsp2[:, d, m:m + 1])
                p.act(sq, sq, AF.Sqrt, scale=-1.0, bias=1.0)
                p.tt(t, ig, u[:, m, u0:u0 + n], ALU.mult)
                p.tt(b, t, sq, ALU.mult)
                if d == 0:
                    init = 0.0 if prev is None else hs[:, prev[0] + prev[1] - 1:prev[0] + prev[1]]
                    p.scan(hs[:, u0:u0 + n], a, b, init)
                else:
                    init = 0.0 if prev is None else prev
                    p.scan(hcur[:, ::-1], a[:, ::-1], b[:, ::-1], init)
                    p.tt(t, hcur, hs[:, u0:u0 + n], ALU.add)
                    ob = tmps[0].bitcast(BF16)[:, 0:n]
                    p.copy(g1[:, 0:1], hcur[:, 0:1])
                    p.tt(ob, t, gl[:, m, u0:u0 + n], ALU.mult)
                    p.dma("sp", out[:, m, u0:u0 + n], ob)
                prev = (u0, n) if d == 0 else g1[:, 0:1]
    return p


def run_lru(inp, hl, hc):
    hT = fm(pad_tokens(hc, hl))
    w = inp["lru_in"][0]
    in_maps = []
    for c in range(NCORES):
        sl = slice(256 * c, 256 * c + 256)
        m = {"hT": hT,
             "w_in": np.ascontiguousarray(np.concatenate([w[:, 2048 + 256 * c:2048 + 256 * c + 256], w[:, sl]], 1)),
             "cw": np.ascontiguousarray(inp["lru_conv_w"][0][:, sl].reshape(4, 2, 128).transpose(2, 1, 0)),
             "cb": np.ascontiguousarray(inp["lru_conv_b"][0][sl].reshape(2, 128).T),
             "gw": np.ascontiguousarray(inp["lru_gate_w"][0][:, :, c].reshape(4, 256, 256)),
             "gb": np.ascontiguousarray(inp["lru_gate_b"][0][:, :, c].reshape(4, 2, 128).transpose(2, 0, 1)),
             "lam": np.ascontiguousarray(inp["lru_lambda"][0][:, sl].reshape(2, 2, 128).transpose(2, 0, 1))}
        in_maps.append(m)
    res = run_prog(build_lru(), in_maps)
    y = np.concatenate([tm(r["oT"]) for r in res], axis=1)
    return y[CTX:], y[:CTX]


def rope_tables():
    rows = SEQ // 64
    row = np.repeat(np.arange(rows, dtype=np.float32), 64)
    col = np.tile(np.arange(64, dtype=np.float32), rows)
    inv = (np.float32(10000.0) ** (-np.arange(16, dtype=np.float32) / np.float32(16))).astype(np.float32)
    ang = np.concatenate([row[:, None] * inv, col[:, None] * inv], axis=-1).astype(np.float32)
    cos, sin = np.cos(ang).astype(np.float32), np.sin(ang).astype(np.float32)
    C2 = np.concatenate([cos, cos], 1).T
    S2 = np.concatenate([-sin, sin], 1).T
    return np.ascontiguousarray(C2), np.ascontiguousarray(S2)


SW64 = np.concatenate([np.arange(32, 64), np.arange(0, 32)])


def build_swa():
    p = Prog()
    hin = p.dram("hT", [128, KC, TOK], BF16, "ExternalInput")
    wq_in = p.dram("wq", [2, D, 256], F32, "ExternalInput")
    wk_in = p.dram("wk", [2, D, 64], F32, "ExternalInput")
    wv_in = p.dram("wv", [D, 64], F32, "ExternalInput")
    bq_in = p.dram("bq", [64, 2, 4], F32, "ExternalInput")
    bk_in = p.dram("bk", [64, 2], F32, "ExternalInput")
    bv_in = p.dram("bv", [128, 64], F32, "ExternalInput")
    c2_in = p.dram("c2", [64, SEQ], F32, "ExternalInput")
    s2_in = p.dram("s2", [64, SEQ], F32, "ExternalInput")
    sk_in = p.dram("sink", [128, 4], F32, "ExternalInput")
    mk_in = p.dram("masks", [128, 2, 128], BF16, "ExternalInput")
    out = p.dram("oT", [256, SEQ], BF16, "ExternalOutput")

    wq = p.sbuf("wqs", [128, 2, KC, 256], BF16)
    wk = p.sbuf("wks", [128, 2, KC, 64], BF16)
    wv = p.sbuf("wvs", [128, KC, 64], BF16)
    bq = p.sbuf("bqs", [64, 2, 4], F32)
    bk = p.sbuf("bks", [64, 2], F32)
    bv = p.sbuf("bvs", [128, 64], F32)
    sk = p.sbuf("sks", [128, 4], F32)
    mk = p.sbuf("mks", [128, 2, 128], BF16)
    ones = p.sbuf("ones", [128, 64], BF16)
    kT = p.sbuf("kT", [64, TOK], BF16)
    qT = p.sbuf("qT", [64, 4, 512], BF16)
    qTp = p.sbuf("qTp", [64, 4, 512], BF16)
    V = p.sbuf("V", [128, TOK // 128, 64], BF16)
    hblk = [p.sbuf(f"hb{i}", [128, KC, 512], BF16) for i in range(2)]
    c2 = [p.sbuf(f"c2{i}", [64, 512], F32) for i in range(2)]
    s2 = [p.sbuf(f"s2{i}", [64, 512], F32) for i in range(2)]
    t1 = p.sbuf("t1", [64, 512], F32)
    t2 = p.sbuf("t2", [64, 512], F32)
    pT = [p.sbuf(f"pT{i}", [128, 512], BF16) for i in range(3)]
    osb = p.sbuf("osb", [64, 4, 512], BF16)
    rs = p.sbuf("rs", [64, 512], F32)
    ps = p.psum("ps", [128, 8, 512], F32)

    for z in range(2):
        p.dma("pool", wq[:, z], wq_in[z].rearrange("(k q) n -> q k n", q=128))
        p.dma("pool", wk[:, z], wk_in[z].rearrange("(k q) n -> q k n", q=128))
    p.dma("pool", wv[:], wv_in.rearrange("(k q) n -> q k n", q=128))
    for (d_, s_) in ((bq, bq_in), (bk, bk_in), (bv, bv_in), (sk, sk_in), (mk, mk_in)):
        p.dma("sp", d_[:], s_)
    p.memset(ones[:], 1.0)
    p.act(sk[:], sk[:], AF.Exp)

    def rope(dst, pa, pb, b0, b1, cc, ss, n, plain=None):
        p.act(t1[:, 0:n], pa, AF.Identity, bias=b0)
        if plain is not None:
            p.copy(plain, t1[:, 0:n], eng="pool")
        p.tt(t1[:, 0:n], t1[:, 0:n], cc[:, 0:n], ALU.mult)
        p.act(t2[:, 0:n], pb, AF.Identity, bias=b1)
        p.tt(t2[:, 0:n], t2[:, 0:n], ss[:, 0:n], ALU.mult)
        p.tt(dst, t1[:, 0:n], t2[:, 0:n], ALU.add)

    nb0 = 0
    for bi, (pc, u0, n) in enumerate(tok_blocks(512)):
        hb = hblk[nb0 % 2]
        p.dma("sp", hb[:, :, 0:n], hin[:, :, u0:u0 + n])
        lat = u0 >= CTX
        if lat:
            l0 = u0 - CTX
            p.dma("sp", c2[nb0 % 2][:, 0:n], c2_in[:, l0:l0 + n])
            p.dma("sp", s2[nb0 % 2][:, 0:n], s2_in[:, l0:l0 + n])
        for z in range(2 if lat else 1):
            acc = ps[0:64, z, 0:n]
            for k in range(KC):
                p.mm(acc, wk[:, z, k, :], hb[:, k, 0:n], start=(k == 0), stop=(k == KC - 1))
        if not lat:
            p.act(kT[:, u0:u0 + n], ps[0:64, 0, 0:n], AF.Identity, bias=bk[:, 0:1])
        else:
            rope(kT[:, u0:u0 + n], ps[0:64, 0, 0:n], ps[0:64, 1, 0:n], bk[:, 0:1], bk[:, 1:2],
                 c2[nb0 % 2], s2[nb0 % 2], n)
        for ti in range(n // 128):
            acc = ps[:, 4 + ti % 2, 0:64]
            for k in range(KC):
                p.mm(acc, hb[:, k, ti * 128:(ti + 1) * 128], wv[:, k, :], start=(k == 0), stop=(k == KC - 1))
            p.tt(V[:, u0 // 128 + ti, :], acc, bv[:], ALU.add)
        nb0 += 1

    NB = SEQ // 128
    np_ = 0
    for blk in range(SEQ // 512):
        hb = hblk[nb0 % 2]
        l0 = blk * 512
        p.dma("sp", hb[:], hin[:, :, CTX + l0:CTX + l0 + 512])
        p.dma("sp", c2[nb0 % 2][:], c2_in[:, l0:l0 + 512])
        p.dma("sp", s2[nb0 % 2][:], s2_in[:, l0:l0 + 512])
        for h in range(4):
            for z in range(2):
                acc = ps[0:64, 2 + z, :]
                for k in range(KC):
                    p.mm(acc, wq[:, z, k, h * 64:(h + 1) * 64], hb[:, k, :], start=(k == 0), stop=(k == KC - 1))
            rope(qT[:, h, :], ps[0:64, 2, :], ps[0:64, 3, :], bq[:, 0, h:h + 1], bq[:, 1, h:h + 1],
                 c2[nb0 % 2], s2[nb0 % 2], 512, plain=qTp[:, h, :])
        nb0 += 1
        for bq_ in range(4):
            b = blk * 4 + bq_
            tiles = [(0, None), (1, None)]
            if b > 0:
                tiles.append((2 + b - 1, 0))
            tiles.append((2 + b, None))
            if b < NB - 1:
                tiles.append((2 + b + 1, 1))
            oacc = ps[0:64, 6, :]
            sacc = ps[0:64, 7, :]
            for ti, (kt, msk) in enumerate(tiles):
                sT = ps[:, np_ % 2, :]
                qsrc = qTp if kt < 2 else qT
                for h in range(4):
                    p.mm(sT[:, h * 128:(h + 1) * 128], kT[:, kt * 128:(kt + 1) * 128],
                         qsrc[:, h, bq_ * 128:(bq_ + 1) * 128], start=True, stop=True)
                pt = pT[np_ % 3]
                p.act(pt[:], sT, AF.Exp, scale=0.125)
                if msk is not None:
                    for h in range(4):
                        p.tt(pt[:, h * 128:(h + 1) * 128], pt[:, h * 128:(h + 1) * 128], mk[:, msk, :], ALU.mult)
                p.mm(oacc, V[:, kt, :], pt[:], start=(ti == 0), stop=(ti == len(tiles) - 1))
                p.mm(sacc, ones[:], pt[:], start=(ti == 0), stop=(ti == len(tiles) - 1))
                np_ += 1
            for h in range(4):
                p.ts(rs[:, h * 128:(h + 1) * 128], sacc[:, h * 128:(h + 1) * 128], sk[0:64, h:h + 1], None, ALU.add)
            p.recip(rs[:], rs[:])
            for h in range(4):
                p.tt(osb[:, h, bq_ * 128:(bq_ + 1) * 128], oacc[:, h * 128:(h + 1) * 128],
                     rs[:, h * 128:(h + 1) * 128], ALU.mult)
        for h in range(4):
            p.dma("sp", out[h * 64:(h + 1) * 64, l0:l0 + 512], osb[:, h, :])
    return p


def swa_masks():
    k = np.arange(128)[:, None]
    q = np.arange(128)[None, :]
    prev = (k >= q)
    nxt = (k <= q)
    return np.stack([prev, nxt], 1).astype(NPBF)


def run_swa(inp, hl, hc):
    hT = fm(np.concatenate([hc, hl], 0))
    W, Bq = inp["swa_qkv"][0], inp["swa_qkv_b"][0]
    C2, S2 = rope_tables()
    msk = swa_masks()
    in_maps = []
    for c in range(NCORES):
        qc = np.arange(256 * c, 256 * c + 256)
        qsw = (qc.reshape(4, 64)[:, SW64]).reshape(-1)
        kc_ = np.arange(2048 + 64 * c, 2048 + 64 * c + 64)
        ksw = kc_[SW64]
        vc = np.arange(2048 + 512 + 64 * c, 2048 + 512 + 64 * c + 64)
        m = {"hT": hT,
             "wq": np.ascontiguousarray(np.stack([W[:, qc], W[:, qsw]])),
             "wk": np.ascontiguousarray(np.stack([W[:, kc_], W[:, ksw]])),
             "wv": np.ascontiguousarray(W[:, vc]),
             "bq": np.ascontiguousarray(np.stack([Bq[qc].reshape(4, 64), Bq[qsw].reshape(4, 64)]).transpose(2, 0, 1)),
             "bk": np.ascontiguousarray(np.stack([Bq[kc_], Bq[ksw]], 1)),
             "bv": np.ascontiguousarray(np.broadcast_to(Bq[vc], (128, 64))),
             "c2": C2, "s2": S2,
             "sink": np.ascontiguousarray(np.broadcast_to(inp["swa_sink"][0][4 * c:4 * c + 4], (128, 4))),
             "masks": msk}
        in_maps.append(m)
    res = run_prog(build_swa(), in_maps)
    y = np.concatenate([r["oT"] for r in res], 0)
    return np.ascontiguousarray(y.T)


NTOK_C = NL + NCX
PBLK = ((0, 512), (512, 1024), (1024, NTOK_C))


def build_mla_pre():
    p = Prog()
    hin = p.dram("hT", [128, KC, NTOK_C], BF16, "ExternalInput")
    w_in = p.dram("w", [D, 1408], F32, "ExternalInput")
    g_in = p.dram("g", [128, 10], F32, "ExternalInput")
    out = p.dram("latT", [128, 12, NTOK_C], BF16, "ExternalOutput")
    hT = p.sbuf("hTs", [128, KC, NTOK_C], BF16)
    w = p.sbuf("ws", [128, KC, 1408], BF16)
    g = p.sbuf("gs", [128, 10], F32)
    a = p.sbuf("a", [128, 10, NTOK_C], F32)
    lat = p.sbuf("lat", [128, 12, NTOK_C], BF16)
    ones = p.sbuf("ones", [128, 128], BF16)
    sqb = p.sbuf("sqb", [128, 2, 512], BF16)
    rstd = p.sbuf("rstd", [128, 2, NTOK_C], F32)
    tmp = p.sbuf("tmp", [128, NTOK_C], F32)
    ps = p.psum("ps", [128, 8, 512], F32)
    p.dma("sp", hT[:], hin)
    wv_ = w_in.rearrange("(k q) n -> q k n", q=128)
    for j in range(0, 1408, 352):
        p.dma("pool", w[:, :, j:j + 352], wv_[:, :, j:j + 352])
    p.dma("sp", g[:], g_in)
    p.memset(ones[:], 1.0)
    p.memset(lat[:, 10:12, :], 0.0)
    n = 0
    for oc in range(12):
        for (c0, c1) in PBLK:
            if oc < 10:
                acc = ps[:, n % 4, 0:c1 - c0]
                for k in range(KC):
                    p.mm(acc, w[:, k, oc * 128:(oc + 1) * 128], hT[:, k, c0:c1], start=(k == 0), stop=(k == KC - 1))
                p.copy(a[:, oc, c0:c1], acc, eng="act")
            else:
                acc = ps[0:64, n % 4, 0:c1 - c0]
                w0 = 1280 + (oc - 10) * 64
                for k in range(KC):
                    p.mm(acc, w[:, k, w0:w0 + 64], hT[:, k, c0:c1], start=(k == 0), stop=(k == KC - 1))
                p.copy(lat[0:64, oc, c0:c1], acc, eng="act")
            n += 1
    for gi, (o0, o1) in enumerate(((0, 6), (6, 10))):
        dim = (o1 - o0) * 128
        for bi, (c0, c1) in enumerate(PBLK):
            acc = ps[:, 4 + bi, 0:c1 - c0]
            for oc in range(o0, o1):
                sq = sqb[:, oc % 2, 0:c1 - c0]
                if oc % 2 == 0:
                    p.act(sq, a[:, oc, c0:c1], AF.Square)
                else:
                    p.tt(sq, a[:, oc, c0:c1], a[:, oc, c0:c1], ALU.mult)
                p.mm(acc, ones[:], sq, start=(oc == o0), stop=(oc == o1 - 1))
            p.ts(rstd[:, gi, c0:c1], acc, 1.0 / dim, EPS, ALU.mult, ALU.add)
        p.act(rstd[:, gi, :], rstd[:, gi, :], AF.Sqrt)
        p.recip(rstd[:, gi, :], rstd[:, gi, :])
        for oc in range(o0, o1):
            p.tt(tmp[:], a[:, oc, :], rstd[:, gi, :], ALU.mult)
            p.ts(lat[:, oc, :], tmp[:], g[:, oc:oc + 1], None, ALU.mult)
    p.dma("sp", out, lat[:])
    return p


def run_mla_pre(inp, hl, hc):
    W = inp["mla_in"][0]
    w = np.ascontiguousarray(np.concatenate([W, W[:, 1280 + SW64]], 1))
    g = np.ascontiguousarray(np.concatenate([inp["mla_q_norm"][0], inp["mla_kv_norm"][0]]).reshape(10, 128).T)
    in_maps = []
    for c in range(NCORES):
        h = np.concatenate([hl[c * NL:(c + 1) * NL], hc[c * NCX:(c + 1) * NCX]], 0)
        in_maps.append({"hT": fm(h), "w": w, "g": g})
    res = run_prog(build_mla_pre(), in_maps)
    lat = [r["latT"] for r in res]
    latc = np.concatenate([x[:, :, NL:] for x in lat], 2)
    latl = np.concatenate([x[:, :, :NL] for x in lat], 2)
    return np.ascontiguousarray(np.concatenate([latc, latl], 2))


MLA_SCALE = 192 ** -0.5


def build_mla(dbg=None):
    p = Prog()
    lin = p.dram("latT", [128, 12, TOK], BF16, "ExternalInput")
    wqn_in = p.dram("wqn", [2, 768, 128], F32, "ExternalInput")
    wqr_in = p.dram("wqr", [2, 2, 768, 64], F32, "ExternalInput")
    wk_in = p.dram("wk", [2, 512, 128], F32, "ExternalInput")
    wv_in = p.dram("wv", [2, 512, 128], F32, "ExternalInput")
    c2_in = p.dram("c2", [64, SEQ], F32, "ExternalInput")
    s2_in = p.dram("s2", [64, SEQ], F32, "ExternalInput")
    out = p.dram("oT", [256, TOK], BF16, "ExternalOutput")

    wqn = p.sbuf("wqn_s", [128, 2, 6, 128], BF16)
    wqr = p.sbuf("wqr_s", [128, 4, 6, 64], BF16)
    wk = p.sbuf("wk_s", [128, 2, 4, 128], BF16)
    wv = p.sbuf("wv_s", [128, 2, 4, 128], BF16)
    ones = p.sbuf("ones", [128, 128], BF16)
    kTn = p.sbuf("kTn", [128, 2, TOK], BF16)
    krT = p.sbuf("krT", [128, TOK], BF16)
    V = p.sbuf("V", [128, 2, TOK // 128, 128], BF16)
    blk = [p.sbuf(f"blk{i}", [128, 6, 512], BF16) for i in range(2)]
    c2 = [p.sbuf(f"c2b{i}", [64, 512], F32) for i in range(2)]
    s2 = [p.sbuf(f"s2b{i}", [64, 512], F32) for i in range(2)]
    t1 = p.sbuf("t1", [64, 512], F32)
    t2 = p.sbuf("t2", [64, 512], F32)
    qn = p.sbuf("qn", [128, 512], BF16)
    qp = p.sbuf("qp", [128, 512], BF16)
    qr = p.sbuf("qr", [128, 512], BF16)
    pT = [p.sbuf(f"pT{i}", [128, 512], BF16) for i in range(3)]
    rs = p.sbuf("rs", [128, 512], F32)
    ob = [p.sbuf(f"ob{i}", [128, 512], BF16) for i in range(2)]
    ps = p.psum("ps", [128, 8, 512], F32)

    for h in range(2):
        p.dma("pool", wqn[:, h], wqn_in[h].rearrange("(k q) n -> q k n", q=128))
        for z in range(2):
            p.dma("pool", wqr[:, 2 * h + z], wqr_in[h, z].rearrange("(k q) n -> q k n", q=128))
        p.dma("pool", wk[:, h], wk_in[h].rearrange("(k q) n -> q k n", q=128))
        p.dma("pool", wv[:, h], wv_in[h].rearrange("(k q) n -> q k n", q=128))
    p.memset(ones[:], 1.0)
    p.memset(krT[64:128, :], 0.0)
    p.memset(qp[64:128, :], 0.0)
    p.memset(qr[64:128, :], 0.0)

    nb = 0
    for (pc, u0, n) in tok_blocks(512):
        b = blk[nb % 2]
        p.dma("sp", b[:, :, 0:n], lin[:, 6:12, u0:u0 + n])
        lat = u0 >= CTX
        for h in range(2):
            acc = ps[:, h, 0:n]
            for k in range(4):
                p.mm(acc, wk[:, h, k, :], b[:, k, 0:n], start=(k == 0), stop=(k == 3))
            p.copy(kTn[:, h, u0:u0 + n], acc, eng="act")
            for ti in range(n // 128):
                acc = ps[:, 2 + (2 * ti + h) % 4, 0:128]
                for k in range(4):
                    p.mm(acc, b[:, k, ti * 128:(ti + 1) * 128], wv[:, h, k, :], start=(k == 0), stop=(k == 3))
                p.copy(V[:, h, u0 // 128 + ti, :], acc)
        if not lat:
            p.copy(krT[0:64, u0:u0 + n], b[0:64, 4, 0:n])
        else:
            l0 = u0 - CTX
            p.dma("sp", c2[nb % 2][:, 0:n], c2_in[:, l0:l0 + n])
            p.dma("sp", s2[nb % 2][:, 0:n], s2_in[:, l0:l0 + n])
            p.tt(t1[:, 0:n], b[0:64, 4, 0:n], c2[nb % 2][:, 0:n], ALU.mult)
            p.tt(t2[:, 0:n], b[0:64, 5, 0:n], s2[nb % 2][:, 0:n], ALU.mult)
            p.tt(krT[0:64, u0:u0 + n], t1[:, 0:n], t2[:, 0:n], ALU.add)
        nb += 1

    if dbg == 0:
        p.dma("sp", out[0:128, 0:TOK], kTn[:, 0, :])
        p.dma("sp", out[128:256, 0:TOK], krT[:, :])
        return p
    groups = [(0, CTX, 2)] + [(CTX + i * 512, 512, TOK // 128) for i in range(SEQ // 512)]
    npi = 0
    no = 0
    for (u0, n, nkt) in groups:
        b = blk[nb % 2]
        lat = u0 >= CTX
        p.dma("sp", b[:, :, 0:n], lin[:, 0:6, u0:u0 + n])
        if lat:
            l0 = u0 - CTX
            p.dma("sp", c2[nb % 2][:, 0:n], c2_in[:, l0:l0 + n])
            p.dma("sp", s2[nb % 2][:, 0:n], s2_in[:, l0:l0 + n])
        for h in range(2):
            acc = ps[:, 0, 0:n]
            for k in range(6):
                p.mm(acc, wqn[:, h, k, :], b[:, k, 0:n], start=(k == 0), stop=(k == 5))
            p.copy(qn[:, 0:n], acc, eng="act")
            for z in range(2 if lat else 1):
                acc = ps[0:64, 1 + z, 0:n]
                for k in range(6):
                    p.mm(acc, wqr[:, 2 * h + z, k, :], b[:, k, 0:n], start=(k == 0), stop=(k == 5))
            p.copy(qp[0:64, 0:n], ps[0:64, 1, 0:n], eng="act")
            if lat and dbg in (5, 6):
                p.copy(qr[0:64, 0:n], ps[0:64, 2, 0:n], eng="act")
            elif lat:
                p.copy(t1[:, 0:n], ps[0:64, 1, 0:n], eng="act")
                p.tt(t1[:, 0:n], t1[:, 0:n], c2[nb % 2][:, 0:n], ALU.mult)
                p.copy(t2[:, 0:n], ps[0:64, 2, 0:n], eng="act")
                p.tt(t2[:, 0:n], t2[:, 0:n], s2[nb % 2][:, 0:n], ALU.mult)
                p.tt(qr[0:64, 0:n], t1[:, 0:n], t2[:, 0:n], ALU.add)
            oacc = ps[:, 6, 0:n]
            sacc = ps[:, 7, 0:n]
            for kt in range(nkt if dbg not in (4, 6) else 4):
                sT = ps[:, 3 + npi % 3, 0:n]
                p.mm(sT, kTn[:, h, kt * 128:(kt + 1) * 128], qn[:, 0:n], start=True, stop=False)
                p.mm(sT, krT[:, kt * 128:(kt + 1) * 128], (qp if kt < 2 else qr)[:, 0:n], start=False, stop=True)
                pt = pT[npi % 3][:, 0:n]
                p.act(pt, sT, AF.Exp, scale=MLA_SCALE)
                nk_ = nkt if dbg not in (4, 6) else 4
                p.mm(oacc, V[:, h, kt, :], pt, start=(kt == 0), stop=(kt == nk_ - 1))
                p.mm(sacc, ones[:], pt, start=(kt == 0), stop=(kt == nk_ - 1))
                npi += 1
            p.recip(rs[:, 0:n], sacc)
            o = ob[no % 2]
            p.tt(o[:, 0:n], oacc, rs[:, 0:n], ALU.mult)
            p.dma("sp", out[h * 128:(h + 1) * 128, u0:u0 + n], o[:, 0:n])
            no += 1
        nb += 1
        if dbg == 1 or (dbg in (2, 4, 5, 6) and u0 >= CTX) or (dbg == 3 and u0 >= CTX + 1024):
            break
    return p


def run_mla(inp, latT, dbg=None):
    Wq, Wkv = inp["mla_q_up"][0], inp["mla_kv_up"][0]
    C2, S2 = rope_tables()
    in_maps = []
    for c in range(NCORES):
        hs = (2 * c, 2 * c + 1)
        wqn = np.stack([Wq[:, h * 192:h * 192 + 128] for h in hs])
        wqr = np.stack([np.stack([Wq[:, h * 192 + 128:h * 192 + 192], Wq[:, h * 192 + 128 + SW64]]) for h in hs])
        wk = np.stack([Wkv[:, h * 256:h * 256 + 128] for h in hs])
        wv = np.stack([Wkv[:, h * 256 + 128:h * 256 + 256] for h in hs])
        in_maps.append({"latT": latT, "wqn": np.ascontiguousarray(wqn), "wqr": np.ascontiguousarray(wqr),
                        "wk": np.ascontiguousarray(wk), "wv": np.ascontiguousarray(wv), "c2": C2, "s2": S2})
    res = run_prog(build_mla(dbg), in_maps)
    y = np.concatenate([r["oT"] for r in res], 0)
    y = np.ascontiguousarray(y.T)
    return y[CTX:], y[:CTX]


NCH = TOK // 128


def build_ssd():
    p = Prog()
    hin = p.dram("hT", [128, KC, TPAD], BF16, "ExternalInput")
    wxbc_in = p.dram("wxbc", [D, 768], F32, "ExternalInput")
    wz_in = p.dram("wz", [D, 512], F32, "ExternalInput")
    wdt_in = p.dram("wdt", [D, 16], F32, "ExternalInput")
    cw_in = p.dram("cw", [128, 6, 4], F32, "ExternalInput")
    cb_in = p.dram("cb", [128, 6], F32, "ExternalInput")
    hv_in = p.dram("hv", [128, 3, 16], F32, "ExternalInput")
    nw_in = p.dram("nw", [128, 512], F32, "ExternalInput")
    cst_in = p.dram("cst", [128, 6, 128], F32, "ExternalInput")
    idb_in = p.dram("idb", [128, 128], BF16, "ExternalInput")
    out = p.dram("y", [TOK, 512], BF16, "ExternalOutput")
    zscr = p.dram("zscr", [TOK, 512], BF16, "Internal")
    yscr = p.dram("yscr", [TOK, 512], F32, "Internal")

    xs = p.sbuf("xs", [128, NCH, 512], BF16)
    Btok = p.sbuf("Btok", [128, NCH, 128], BF16)
    BT = p.sbuf("BT", [128, TOK], BF16)
    CT = p.sbuf("CT", [128, TOK], BF16)
    dt = p.sbuf("dt", [128, NCH, 16], F32)
    aa = p.sbuf("aa", [128, NCH, 16], F32)
    cw = p.sbuf("cws", [128, 6, 4], F32)
    cb = p.sbuf("cbs", [128, 6], F32)
    hv = p.sbuf("hvs", [128, 3, 16], F32)
    nw = p.sbuf("nws", [128, 512], F32)
    cst = p.sbuf("csts", [128, 6, 128], F32)
    idb = p.sbuf("idbs", [128, 128], BF16)
    pre = p.sbuf("pre", [128, 520], F32)
    tmpc = p.sbuf("tmpc", [128, 512], F32)
    ARENA = 32 * 1024
    ar = p.sbuf("arena", [128, ARENA], BF16)
    ps = p.psum("ps", [128, 8, 512], F32)
    for (d_, s_) in ((cw, cw_in), (cb, cb_in), (hv, hv_in), (nw, nw_in), (cst, cst_in), (idb, idb_in)):
        p.dma("sp", d_[:], s_)
    U = [cst[:, 0, :], cst[:, 1, :]]
    NEG = [cst[:, 2, :], cst[:, 3, :]]
    identf = cst[:, 4, :]
    onesf = cst[:, 5, :]
    p.act(hv[:, 1, :], hv[:, 1, :], AF.Exp)
    p.ts(hv[:, 1, :], hv[:, 1, :], -1.0, None, ALU.mult)

    o = 0
    wx = ar[:, o:o + KC * 768].rearrange("p (k n) -> p k n", n=768); o += KC * 768
    wd = ar[:, o:o + KC * 16].rearrange("p (k n) -> p k n", n=16); o += KC * 16
    hblk = []
    for i in range(2):
        hblk.append(ar[:, o:o + KC * 515].rearrange("p (k n) -> p k n", n=515)); o += KC * 515
    xT = ar[:, o:o + 6 * 512].rearrange("p (k n) -> p k n", n=512); o += 6 * 512
    assert o <= ARENA
    p.dma("pool", wx, wxbc_in.rearrange("(k q) n -> q k n", q=128))
    p.dma("pool", wd, wdt_in.rearrange("(k q) n -> q k n", q=128))
    sp1 = p.sbuf("sp1", [128, 16], F32)
    sp2 = p.sbuf("sp2", [128, 16], F32)
    for bi, (pc, u0, n) in enumerate(tok_blocks(512)):
        hb = hblk[bi % 2]
        ncol = n + 3
        p.dma("sp", hb[:, :, 0:ncol], hin[:, :, pc - 1:pc - 1 + ncol])
        for m in range(6):
            pieces = [(0, min(512, ncol))] + ([(512, ncol)] if ncol > 512 else [])
            for pi, (c0, c1) in enumerate(pieces):
                acc = ps[:, (2 * m + pi) % 4, 0:c1 - c0]
                for k in range(KC):
                    p.mm(acc, wx[:, k, m * 128:(m + 1) * 128], hb[:, k, c0:c1], start=(k == 0), stop=(k == KC - 1))
                p.copy(pre[:, c0:c1], acc, eng="act")
            emit_conv4(p, pre, tmpc[:, 0:n], cw, cb, m, n, tmpc)
            dst = xT[:, m, 0:n] if m < 5 else CT[:, u0:u0 + n]
            p.act(pre[:, 0:n], tmpc[:, 0:n], AF.Sigmoid)
            p.tt(dst, tmpc[:, 0:n], pre[:, 0:n], ALU.mult)
            if m == 4:
                p.copy(BT[:, u0:u0 + n], xT[:, 4, 0:n], eng="pool")
        for ti in range(n // 128):
            ci = u0 // 128 + ti
            tp = ps[:, 4 + ti % 2, :].bitcast(BF16)
            for m in range(5):
                p.transpose(tp[:, m * 128:(m + 1) * 128], xT[:, m, ti * 128:(ti + 1) * 128], idb[:])
            p.copy(xs[:, ci, :], tp[:, 0:512])
            p.copy(Btok[:, ci, :], tp[:, 512:640], eng="act")
            acc = ps[:, 6 + ti % 2, 0:16]
            for k in range(KC):
                p.mm(acc, hb[:, k, 1 + ti * 128:1 + (ti + 1) * 128], wd[:, k, :], start=(k == 0), stop=(k == KC - 1))
            p.tt(sp1[:], acc, hv[:, 0, :], ALU.add)
            p.act(sp2[:], sp1[:], AF.Abs)
            p.act(sp2[:], sp2[:], AF.Exp, scale=-1.0)
            p.act(sp2[:], sp2[:], AF.Ln, bias=1.0)
            p.ts(sp1[:], sp1[:], 0.0, None, ALU.max)
            p.tt(dt[:, ci, :], sp1[:], sp2[:], ALU.add)
            p.tt(aa[:, ci, :], dt[:, ci, :], hv[:, 1, :], ALU.mult)

    o = 0
    wz = ar[:, o:o + KC * 512].rearrange("p (k n) -> p k n", n=512); o += KC * 512
    hb2 = []
    for i in range(2):
        hb2.append(ar[:, o:o + KC * 512].rearrange("p (k n) -> p k n", n=512)); o += KC * 512
    zb = [ar[:, o + i * 512:o + (i + 1) * 512] for i in range(2)]; o += 1024
    assert o <= ARENA
    p.dma("pool", wz, wz_in.rearrange("(k q) n -> q k n", q=128))
    nz = 0
    for bi, (pc, u0, n) in enumerate(tok_blocks(512)):
        hb = hb2[bi % 2]
        p.dma("sp", hb[:, :, 0:n], hin[:, :, pc:pc + n])
        for ti in range(n // 128):
            acc = ps[:, nz % 2, :]
            for k in range(KC):
                p.mm(acc, hb[:, k, ti * 128:(ti + 1) * 128], wz[:, k, :], start=(k == 0), stop=(k == KC - 1))
            p.act(tmpc[:], acc, AF.Sigmoid)
            p.tt(zb[nz % 2], tmpc[:], acc, ALU.mult)
            p.dma("sp", zscr[u0 + ti * 128:u0 + (ti + 1) * 128, :], zb[nz % 2])
            nz += 1

    arf = ar[:, :].bitcast(F32)
    o = 0

    def fa(n):
        nonlocal o
        v = arf[:, o:o + n]
        o += n
        return v
    hS = fa(512)
    hSb = ar[:, 2 * o:2 * o + 512]; o += 256
    abc = [fa(128) for _ in range(2)]
    Ee = [fa(128) for _ in range(2)]
    cbT = fa(128)
    MT = [ar[:, 2 * o + i * 128:2 * o + (i + 1) * 128] for i in range(2)]; o += 128
    xdt = ar[:, 2 * o:2 * o + 512]; o += 256
    xdw = ar[:, 2 * o:2 * o + 512]; o += 256
    csn = fa(8)
    ecs = fa(8)
    wv_ = fa(8)
    dw = fa(8)
    dec = fa(8)
    yd = fa(512)
    yf = [fa(512) for _ in range(2)]
    zl = [ar[:, 2 * o + i * 512:2 * o + (i + 1) * 512] for i in range(2)]; o += 512
    gg = fa(512)
    ss = fa(2)
    ob = [ar[:, 2 * o + i * 512:2 * o + (i + 1) * 512] for i in range(2)]; o += 512
    assert 2 * o <= ARENA
    for d in range(2):
        order = list(range(NCH)) if d == 0 else [1, 0] + list(range(NCH - 1, 1, -1))
        p.memset(hS, 0.0)
        p.memset(hSb, 0.0)
        for it, ci in enumerate(order):
            a_d = aa[:, ci, d * 8:(d + 1) * 8]
            dt_d = dt[:, ci, d * 8:(d + 1) * 8]
            t0 = ci * 128
            if d == 1:
                p.dma("sp", yf[it % 2], yscr[t0:t0 + 128, :])
                p.dma("sp", zl[it % 2], zscr[t0:t0 + 128, :])
            p.mm(ps[:, 0, 0:8], U[d], a_d, start=True, stop=True)
            p.mm(ps[:, 0, 8:16], onesf, a_d, start=True, stop=True)
            p.ts(csn, ps[:, 0, 0:8], -1.0, None, ALU.mult)
            p.act(ecs, ps[:, 0, 0:8], AF.Exp)
            p.tt(wv_, ps[:, 0, 8:16], csn, ALU.add)
            p.act(wv_, wv_, AF.Exp)
            p.tt(dw, wv_, dt_d, ALU.mult)
            p.act(dec, ps[:, 0, 8:16], AF.Exp)
            p.mm(ps[:, 1, 0:128], BT[:, t0:t0 + 128], CT[:, t0:t0 + 128], start=True, stop=True)
            p.copy(cbT, ps[:, 1, 0:128], eng="act")
            p.mm(ps[:, 5, :], CT[:, t0:t0 + 128], hSb, start=True, stop=True)
            for e in range(8):
                es = slice(e * 64, (e + 1) * 64)
                p.act(xdt[:, es], xs[:, ci, es], AF.Copy, scale=dt_d[:, e:e + 1])
                p.act(xdw[:, es], xs[:, ci, es], AF.Copy, scale=dw[:, e:e + 1])
            for e in range(8):
                es = slice(e * 64, (e + 1) * 64)
                ab = abc[e % 2]
                p.act(ab, onesf, AF.Copy, scale=a_d[:, e:e + 1])
                R = ps[:, 2 + e % 2, 0:128]
                p.mm(R, ab, U[d], start=True, stop=False)
                p.mm(R, identf, NEG[d], start=False, stop=True)
                p.act(Ee[e % 2], R, AF.Exp, bias=csn[:, e:e + 1])
                p.tt(MT[e % 2], Ee[e % 2], cbT, ALU.mult)
                p.mm(ps[:, 4, es], MT[e % 2], xdt[:, es], start=True, stop=True)
            p.mm(ps[:, 6, :], Btok[:, ci, :], xdw, start=True, stop=True)
            p.copy(yd, ps[:, 4, :], eng="act")
            ydst = yd
            for e in range(8):
                es = slice(e * 64, (e + 1) * 64)
                p.stt(yd[:, es], ps[:, 5, es], ecs[:, e:e + 1], yd[:, es], ALU.mult, ALU.add)
                p.stt(yd[:, es], xs[:, ci, es], hv[:, 2, d * 8 + e:d * 8 + e + 1], yd[:, es], ALU.mult, ALU.add)
            for e in range(8):
                es = slice(e * 64, (e + 1) * 64)
                p.stt(hS[:, es], hS[:, es], dec[:, e:e + 1], ps[:, 6, es], ALU.mult, ALU.add)
            p.copy(hSb, hS, eng="pool")
            if d == 0:
                p.dma("sp", yscr[t0:t0 + 128, :], yd)
            else:
                p.tt(yd, yd, yf[it % 2], ALU.add)
                p.tt(yd, yd, zl[it % 2], ALU.mult)
                p.act(gg, yd, AF.Square, accum_out=ss[:, 0:1])
                p.ts(ss[:, 1:2], ss[:, 0:1], 1.0 / 512, EPS, ALU.mult, ALU.add)
                p.act(ss[:, 1:2], ss[:, 1:2], AF.Sqrt)
                p.recip(ss[:, 1:2], ss[:, 1:2])
                p.stt(ob[it % 2], yd, ss[:, 1:2], nw[:], ALU.mult, ALU.mult)
                p.dma("sp", out[t0:t0 + 128, :], ob[it % 2])
    return p


def ssd_consts():
    s = np.arange(128)[:, None]
    l = np.arange(128)[None, :]
    Uf = (s <= l).astype(np.float32)
    Ub = (s >= l).astype(np.float32)
    Nf = np.where(s <= l, 0.0, -30000.0).astype(np.float32)
    Nb = np.where(s >= l, 0.0, -30000.0).astype(np.float32)
    I = np.eye(128, dtype=np.float32)
    O = np.ones((128, 128), np.float32)
    return np.ascontiguousarray(np.stack([Uf, Ub, Nf, Nb, I, O], 1))


def run_ssd(inp, hl, hc):
    hT = fm(pad_tokens(hc, hl))
    W = inp["ssd_in"][0]
    CWf, CBf = inp["ssd_conv_w"][0], inp["ssd_conv_b"][0]
    cst = ssd_consts()
    idb = np.eye(128, dtype=np.float32).astype(NPBF)
    in_maps = []
    for c in range(NCORES):
        ch = np.concatenate([np.arange(512 * c, 512 * c + 512), 4096 + np.arange(128 * c, 128 * c + 128),
                             5120 + np.arange(128 * c, 128 * c + 128)])
        hd = np.arange(8 * c, 8 * c + 8)
        dtc = np.concatenate([4096 + 6144 + hd, 4096 + 6144 + 64 + hd])
        hvv = np.stack([np.concatenate([inp["ssd_dt_bias"][0][0, hd], inp["ssd_dt_bias"][0][1, hd]]),
                        np.concatenate([inp["ssd_a_log"][0][0, hd], inp["ssd_a_log"][0][1, hd]]),
                        np.concatenate([inp["ssd_d"][0][0, hd], inp["ssd_d"][0][1, hd]])])
        m = {"hT": hT,
             "wxbc": np.ascontiguousarray(W[:, 4096 + ch]),
             "wz": np.ascontiguousarray(W[:, 512 * c:512 * c + 512]),
             "wdt": np.ascontiguousarray(W[:, dtc]),
             "cw": np.ascontiguousarray(CWf[:, ch].reshape(4, 6, 128).transpose(2, 1, 0)),
             "cb": np.ascontiguousarray(CBf[ch].reshape(6, 128).T),
             "hv": np.ascontiguousarray(np.broadcast_to(hvv, (128, 3, 16))),
             "nw": np.ascontiguousarray(np.broadcast_to(inp["ssd_norm"][0][512 * c:512 * c + 512], (128, 512))),
             "cst": cst, "idb": idb}
        in_maps.append(m)
    res = run_prog(build_ssd(), in_maps)
    y = np.concatenate([r["y"] for r in res], 1)
    return y[CTX:], y[:CTX]


def kernel(**inputs):
    inp = {k: np.asarray(v) for k, v in inputs.items()}
    mods = run_mods(inp)
    xl, xc = inp["x"][0], inp["ctx"][0]
    o = run_T(inp, mods, xl, xc, None, None, None, None, "h")
    hl, hc = o["hl"], o["hc"]
    yl, yc = run_ssd(inp, hl, hc)
    o = run_T(inp, mods, xl, xc, 0, yl, yc, inp["ssd_out"][0], "h")
    xl, xc, hl, hc = o["xl"], o["xc"], o["hl"], o["hc"]
    yl, yc = run_lru(inp, hl, hc)
    o = run_T(inp, mods, xl, xc, 1, yl, yc, inp["lru_out"][0], "h")
    xl, xc, hl, hc = o["xl"], o["xc"], o["hl"], o["hc"]
    latT = run_mla_pre(inp, hl, hc)
    yl, yc = run_mla(inp, latT)
    o = run_T(inp, mods, xl, xc, 2, yl, yc, inp["mla_out"][0], "h")
    xl, xc, hl, hc = o["xl"], o["xc"], o["hl"], o["hc"]
    yl = run_swa(inp, hl, hc)
    yc = np.zeros((CTX, D), NPBF)
    o = run_T(inp, mods, xl, xc, 3, yl, yc, inp["swa_out"][0], "final")
    return o["f"][None].astype(np.float32)
```
